# Optimizing a Trainium2 kernel written in Bass

```python
import math
import jax, jax.numpy as jnp
from jax import lax
import numpy as np

D_MODEL = 2048
BATCH = 8
SEQ = 2048
DEPTH = 4

N_MIXERS = 2
HEAD_DIM = 128
ROPE_THETA = 10000.0
NORM_EPS = 1e-6
D_FF = 4 * D_MODEL
Q_BLOCK = 128
NEG = -1e30
BIG = 1e30

NSA_HEADS = D_MODEL // HEAD_DIM
NSA_KV_HEADS = 4
NSA_GROUP = NSA_HEADS // NSA_KV_HEADS
CMP_BLOCK = 32
CMP_STRIDE = 16
CMP_HIDDEN = 2 * HEAD_DIM
SLC_BLOCK = 64
SLC_TOPN = 16
SLC_QCHUNK = 64
WIN = 512
NSA_IN = NSA_HEADS * HEAD_DIM + 6 * NSA_KV_HEADS * HEAD_DIM + 3 * NSA_HEADS

DIFF_HEADS = D_MODEL // (2 * HEAD_DIM)
DIFF_IN = 3 * DIFF_HEADS * 2 * HEAD_DIM

kernel_name = "hybrid_nsa_diffattn_sqrelu_adaln"


def rms_norm(x, g):
    xf = x.astype(jnp.float32)
    y = xf * lax.rsqrt(jnp.mean(xf * xf, axis=-1, keepdims=True) + NORM_EPS)
    return (y * g.astype(jnp.float32)).astype(x.dtype)


def rope(x, pos):
    half = x.shape[-1] // 2
    inv = ROPE_THETA ** (-jnp.arange(half, dtype=jnp.float32) / half)
    ang = pos.astype(jnp.float32)[..., None] * inv
    cos = jnp.cos(ang)[:, :, None, :]
    sin = jnp.sin(ang)[:, :, None, :]
    xf = x.astype(jnp.float32)
    x1, x2 = xf[..., :half], xf[..., half:]
    return jnp.concatenate([x1 * cos - x2 * sin, x2 * cos + x1 * sin], axis=-1).astype(x.dtype)


def _compress(blocks, pos_emb, w1, w2):
    hid = jax.nn.silu(jnp.einsum('bnlgd,ldf->bngf', blocks + pos_emb[None, None, :, None, :], w1))
    return jnp.einsum('bngf,fd->bngd', hid, w2)


def _cmp_to_slc_map(n_cmp, n_slc):
    cs = CMP_STRIDE * np.arange(n_cmp)[:, None]
    ss = SLC_BLOCK * np.arange(n_slc)[None, :]
    ov = np.minimum(cs + CMP_BLOCK, ss + SLC_BLOCK) - np.maximum(cs, ss)
    return (np.clip(ov, 0, None) / CMP_STRIDE).astype(np.float32)


def _window_branch(qg, k, v, scale):
    B, T, G, R, dh = qg.shape
    nb = T // Q_BLOCK
    span = WIN + Q_BLOCK
    kp = jnp.pad(k, ((0, 0), (WIN, 0), (0, 0), (0, 0)))
    vp = jnp.pad(v, ((0, 0), (WIN, 0), (0, 0), (0, 0)))

    def blk(n):
        start = n * Q_BLOCK
        qb = lax.dynamic_slice_in_dim(qg, start, Q_BLOCK, axis=1)
        kb = lax.dynamic_slice_in_dim(kp, start, span, axis=1)
        vb = lax.dynamic_slice_in_dim(vp, start, span, axis=1)
        q_tok = start + jnp.arange(Q_BLOCK)
        k_tok = start - WIN + jnp.arange(span)
        rel = q_tok[:, None] - k_tok[None, :]
        mask = (rel >= 0) & (rel < WIN) & (k_tok[None, :] >= 0)
        s = jnp.einsum('bqgrd,bkgd->bqgrk', qb, kb, preferred_element_type=jnp.float32) * scale
        p = jax.nn.softmax(jnp.where(mask[None, :, None, None, :], s, NEG), axis=-1)
        return jnp.einsum('bqgrk,bkgd->bqgrd', p.astype(vb.dtype), vb)

    o = lax.map(blk, jnp.arange(nb))
    return jnp.moveaxis(o, 0, 1).reshape(B, T, G, R, dh)


def _select_branch(qg, k, v, idx, scale):
    B, T, G, R, dh = qg.shape
    ns = T // SLC_BLOCK
    nsel = idx.shape[-1]
    nc = T // SLC_QCHUNK
    k_blk = k.reshape(B, ns, SLC_BLOCK, G, dh).transpose(0, 3, 1, 2, 4)
    v_blk = v.reshape(B, ns, SLC_BLOCK, G, dh).transpose(0, 3, 1, 2, 4)
    g_ar = jnp.arange(G)[None, :, None]
    qc = qg.reshape(B * nc, SLC_QCHUNK, G, R, dh)
    ic = idx.reshape(B * nc, SLC_QCHUNK, G, nsel)
    bid = jnp.repeat(jnp.arange(B), nc)
    cid = jnp.tile(jnp.arange(nc), B)

    def step(args):
        qb, ib, b, ci = args
        kg = k_blk[b][g_ar, ib]
        vg = v_blk[b][g_ar, ib]
        s = jnp.einsum('qgrd,qgnld->qgrnl', qb, kg, preferred_element_type=jnp.float32) * scale
        key_tok = ib[..., None] * SLC_BLOCK + jnp.arange(SLC_BLOCK)
        q_tok = ci * SLC_QCHUNK + jnp.arange(SLC_QCHUNK)
        mask = key_tok <= q_tok[:, None, None, None]
        p = jax.nn.softmax(jnp.where(mask[:, :, None], s, NEG), axis=(-2, -1))
        return jnp.einsum('qgrnl,qgnld->qgrd', p.astype(vg.dtype), vg)

    o = lax.map(step, (qc, ic, bid, cid))
    return o.reshape(B, T, G, R, dh)


def nsa_mixer(h, pos, w_in, w_out, q_g, k_g, cmp_pos, cmp_w1, cmp_w2):
    B, T, _ = h.shape
    H, G, R, dh = NSA_HEADS, NSA_KV_HEADS, NSA_GROUP, HEAD_DIM
    scale = dh ** -0.5
    sizes = [H * dh] + [G * dh] * 6
    splits = [int(s) for s in np.cumsum(sizes)]
    q, kc, vc, ks, vs, kw, vw, gates = jnp.split(h @ w_in, splits, axis=-1)
    q = rope(rms_norm(q.reshape(B, T, H, dh), q_g), pos)
    qg = q.reshape(B, T, G, R, dh)

    n_cmp = (T - CMP_BLOCK) // CMP_STRIDE + 1
    blk_idx = CMP_STRIDE * np.arange(n_cmp)[:, None] + np.arange(CMP_BLOCK)[None, :]
    kc = kc.reshape(B, T, G, dh)
    vc = vc.reshape(B, T, G, dh)
    k_cmp = _compress(kc[:, blk_idx], cmp_pos[0], cmp_w1[0], cmp_w2[0])
    v_cmp = _compress(vc[:, blk_idx], cmp_pos[1], cmp_w1[1], cmp_w2[1])
    k_cmp = rope(rms_norm(k_cmp, k_g[0]), pos[:, blk_idx[:, -1]])
    cmp_end = jnp.asarray(blk_idx[:, -1])
    cmask = cmp_end[None, :] <= jnp.arange(T)[:, None]
    cmask5 = cmask[None, :, None, None, :]
    s = jnp.einsum('btgrd,bngd->btgrn', qg, k_cmp, preferred_element_type=jnp.float32) * scale
    p_cmp = jnp.where(cmask5, jax.nn.softmax(jnp.where(cmask5, s, NEG), axis=-1), 0.0)
    o_cmp = jnp.einsum('btgrn,bngd->btgrd', p_cmp.astype(v_cmp.dtype), v_cmp)

    n_slc = T // SLC_BLOCK
    n_sel = min(SLC_TOPN, n_slc)
    imp = jnp.einsum('btgn,ns->btgs', p_cmp.sum(axis=3), jnp.asarray(_cmp_to_slc_map(n_cmp, n_slc)))
    t_blk = (jnp.arange(T) // SLC_BLOCK)[:, None]
    s_ids = jnp.arange(n_slc)[None, :]
    valid = (s_ids <= t_blk)[None, :, None, :]
    forced = ((s_ids == 0) | (s_ids == t_blk) | (s_ids == t_blk - 1))[None, :, None, :]
    imp = jnp.where(forced, BIG, jnp.where(valid, imp, NEG))
    _, sel_idx = lax.top_k(imp, n_sel)

    ks = rope(rms_norm(ks.reshape(B, T, G, dh), k_g[1]), pos)
    o_slc = _select_branch(qg, ks, vs.reshape(B, T, G, dh), sel_idx, scale)

    kw = rope(rms_norm(kw.reshape(B, T, G, dh), k_g[2]), pos)
    o_win = _window_branch(qg, kw, vw.reshape(B, T, G, dh), scale)

    g = jax.nn.sigmoid(gates.astype(jnp.float32)).astype(h.dtype).reshape(B, T, 3, G, R, 1)
    o = g[:, :, 0] * o_cmp + g[:, :, 1] * o_slc + g[:, :, 2] * o_win
    return o.reshape(B, T, H * dh) @ w_out


def diff_mixer(h, pos, w_in, w_out, q_g, k_g, lam_p, sub_g, lam_init):
    B, T, _ = h.shape
    H, dh = DIFF_HEADS, HEAD_DIM
    scale = dh ** -0.5
    q, k, v = jnp.split(h @ w_in, 3, axis=-1)
    q = rope(rms_norm(q.reshape(B, T, 2 * H, dh), q_g), pos).reshape(B, T, H, 2, dh)
    k = rope(rms_norm(k.reshape(B, T, 2 * H, dh), k_g), pos).reshape(B, T, H, 2, dh)
    v = v.reshape(B, T, H, 2 * dh)
    lp = lam_p.astype(jnp.float32)
    lam = jnp.exp(jnp.sum(lp[0] * lp[1])) - jnp.exp(jnp.sum(lp[2] * lp[3])) + lam_init
    k_tok = jnp.arange(T)

    def blk(n):
        start = n * Q_BLOCK
        qb = lax.dynamic_slice_in_dim(q, start, Q_BLOCK, axis=1)
        s = jnp.einsum('bqhcd,bkhcd->bhcqk', qb, k, preferred_element_type=jnp.float32) * scale
        mask = k_tok[None, :] <= (start + jnp.arange(Q_BLOCK))[:, None]
        p = jax.nn.softmax(jnp.where(mask, s, NEG), axis=-1)
        a = p[:, :, 0] - lam * p[:, :, 1]
        return jnp.einsum('bhqk,bkhe->bqhe', a.astype(v.dtype), v)

    o = lax.map(blk, jnp.arange(T // Q_BLOCK))
    o = jnp.moveaxis(o, 0, 1).reshape(B, T, H, 2 * dh)
    o = rms_norm(o, sub_g) * (1.0 - lam_init)
    return o.reshape(B, T, H * 2 * dh) @ w_out


def sqrelu_mlp(h, w1, w2):
    return jnp.square(jax.nn.relu(h @ w1)) @ w2


def setup_inputs(seed: int = 0) -> dict:
    key = jax.random.key(seed)
    ks = jax.random.split(key, 24)
    f32 = jnp.float32
    D, L = D_MODEL, DEPTH
    n_nsa = (DEPTH + 1) // 2
    n_diff = DEPTH // 2

    def nrm(k, shape, s):
        return jax.random.normal(k, shape, f32) * s

    x = nrm(ks[0], (BATCH, SEQ, D), 1.0)
    c = nrm(ks[1], (BATCH, D), 1.0)
    positions = (jax.random.randint(ks[2], (BATCH, 1), 0, 1024, dtype=jnp.int32)
                 + jnp.arange(SEQ, dtype=jnp.int32)[None, :])
    return {
        "x": x,
        "c": c,
        "positions": positions,
        "ada_w": nrm(ks[3], (L, D, 6 * D), 0.5 * D ** -0.5),
        "ada_b": nrm(ks[4], (L, 6 * D), 0.01),
        "attn_norm_g": 1.0 + nrm(ks[5], (L, D), 0.02),
        "mlp_norm_g": 1.0 + nrm(ks[6], (L, D), 0.02),
        "mlp_w1": nrm(ks[7], (L, D, D_FF), D ** -0.5),
        "mlp_w2": nrm(ks[8], (L, D_FF, D), D_FF ** -0.5),
        "nsa_w_in": nrm(ks[9], (n_nsa, D, NSA_IN), D ** -0.5),
        "nsa_w_out": nrm(ks[10], (n_nsa, NSA_HEADS * HEAD_DIM, D), (NSA_HEADS * HEAD_DIM) ** -0.5),
        "nsa_q_norm": 1.0 + nrm(ks[11], (n_nsa, HEAD_DIM), 0.02),
        "nsa_k_norm": 1.0 + nrm(ks[12], (n_nsa, 3, HEAD_DIM), 0.02),
        "nsa_cmp_pos": nrm(ks[13], (n_nsa, 2, CMP_BLOCK, HEAD_DIM), 0.1),
        "nsa_cmp_w1": nrm(ks[14], (n_nsa, 2, CMP_BLOCK, HEAD_DIM, CMP_HIDDEN), (CMP_BLOCK * HEAD_DIM) ** -0.5),
        "nsa_cmp_w2": nrm(ks[15], (n_nsa, 2, CMP_HIDDEN, HEAD_DIM), CMP_HIDDEN ** -0.5),
        "diff_w_in": nrm(ks[16], (n_diff, D, DIFF_IN), D ** -0.5),
        "diff_w_out": nrm(ks[17], (n_diff, DIFF_HEADS * 2 * HEAD_DIM, D), (DIFF_HEADS * 2 * HEAD_DIM) ** -0.5),
        "diff_q_norm": 1.0 + nrm(ks[18], (n_diff, HEAD_DIM), 0.02),
        "diff_k_norm": 1.0 + nrm(ks[19], (n_diff, HEAD_DIM), 0.02),
        "diff_lambda": nrm(ks[20], (n_diff, 4, HEAD_DIM), 0.1),
        "diff_sub_norm": 1.0 + nrm(ks[21], (n_diff, 2 * HEAD_DIM), 0.02),
    }


def reference(x, c, positions, ada_w, ada_b, attn_norm_g, mlp_norm_g, mlp_w1, mlp_w2,
              nsa_w_in, nsa_w_out, nsa_q_norm, nsa_k_norm, nsa_cmp_pos, nsa_cmp_w1, nsa_cmp_w2,
              diff_w_in, diff_w_out, diff_q_norm, diff_k_norm, diff_lambda, diff_sub_norm):
    mod_all = jnp.einsum('bd,lde->lbe', jax.nn.silu(c), ada_w) + ada_b[:, None, :]
    for i in range(DEPTH):
        sh1, sc1, g1, sh2, sc2, g2 = jnp.split(mod_all[i][:, None, :], 6, axis=-1)
        h = rms_norm(x, attn_norm_g[i]) * (1.0 + sc1) + sh1
        j = i // N_MIXERS
        if i % N_MIXERS == 0:
            y = nsa_mixer(h, positions, nsa_w_in[j], nsa_w_out[j], nsa_q_norm[j], nsa_k_norm[j],
                          nsa_cmp_pos[j], nsa_cmp_w1[j], nsa_cmp_w2[j])
        else:
            lam_init = 0.8 - 0.6 * math.exp(-0.3 * i)
            y = diff_mixer(h, positions, diff_w_in[j], diff_w_out[j], diff_q_norm[j], diff_k_norm[j],
                           diff_lambda[j], diff_sub_norm[j], lam_init)
        x = x + g1 * y
        h = rms_norm(x, mlp_norm_g[i]) * (1.0 + sc2) + sh2
        x = x + g2 * sqrelu_mlp(h, mlp_w1[i], mlp_w2[i])
    return x
```

```python
import contextlib
import math
import numpy as np
import ml_dtypes
import concourse.bass as bass
import concourse.mybir as mybir
from concourse.bass_utils import run_bass_kernel_spmd

F32 = mybir.dt.float32
BF16 = mybir.dt.bfloat16
I32 = mybir.dt.int32
ALU = mybir.AluOpType
AF = mybir.ActivationFunctionType
AX = mybir.AxisListType

SEM_LIMIT = 30000
D = 2048
T = 2048
NT = 16
KC = 16
DFF = 8192
EPS = 1e-6
SCALE = 128 ** -0.5
NCMP = 127
BIGV = 1e30


class Tok:
    __slots__ = ("w", "r")

    def __init__(self):
        self.w = None
        self.r = {}


class FW:
    def __init__(self, nc, stack):
        self.nc = nc
        self.stack = stack
        self.eng = {"pe": nc.tensor, "dve": nc.vector, "act": nc.scalar,
                    "pool": nc.gpsimd, "sp": nc.sync}
        self.esem = {}
        self.ecnt = {}
        self.known = {k: {} for k in self.eng}
        self.nsem = 0
        for k in self.eng:
            self._new_esem(k)
        self.NDS = 6
        self.dsem = {q: [self._sem("d%s%d" % (q, i)) for i in range(self.NDS)] for q in ("sp", "pool")}
        self.dcnt = {q: [0] * self.NDS for q in self.dsem}
        self.drr = {q: 0 for q in self.dsem}
        self.out_events = []
        self.n_ins = 0
        self.n_wait = 0

    def _sem(self, name):
        self.nsem += 1
        return self.stack.enter_context(self.nc.semaphore("%s_%d" % (name, self.nsem)))

    def _new_esem(self, k):
        self.esem[k] = self._sem("e" + k)
        self.ecnt[k] = 0

    def _need(self, k, ev):
        if ev is None:
            return
        sem, val = ev
        if val <= 0:
            return
        kn = self.known[k]
        if kn.get(sem, 0) >= val:
            return
        if k == "pe" and sem is self.esem["pe"]:
            return
        self.eng[k].wait_ge(sem, val)
        self.n_wait += 1
        kn[sem] = val

    def _deps(self, k, reads, writes):
        for t in reads:
            self._need(k, t.w)
        for t in writes:
            self._need(k, t.w)
            for s, v in list(t.r.items()):
                self._need(k, (s, v))

    def _commit(self, ev, reads, writes):
        s, v = ev
        for t in reads:
            if t.r.get(s, 0) < v:
                t.r[s] = v
        for t in writes:
            t.w = ev
            t.r = {}

    def op(self, k, fn, reads=(), writes=()):
        self._deps(k, reads, writes)
        if self.ecnt[k] >= SEM_LIMIT:
            self._new_esem(k)
        ins = fn(self.eng[k])
        self.ecnt[k] += 1
        ins.then_inc(self.esem[k], 1)
        ev = (self.esem[k], self.ecnt[k])
        self._commit(ev, reads, writes)
        self.n_ins += 1
        return ev

    def dma(self, q, out, in_, reads=(), writes=(), is_output=False, **kw):
        self._deps(q, reads, writes)
        i = self.drr[q]
        self.drr[q] = (i + 1) % self.NDS
        if self.dcnt[q][i] + 16 > SEM_LIMIT:
            self._need(q, (self.dsem[q][i], self.dcnt[q][i]))
            self.dsem[q][i] = self._sem("d%s%d" % (q, i))
            self.dcnt[q][i] = 0
        sem = self.dsem[q][i]
        self._need(q, (sem, self.dcnt[q][i]))
        ins = self.eng[q].dma_start(out=out, in_=in_, **kw)
        self.dcnt[q][i] += 16
        ins.then_inc(sem, 16)
        ev = (sem, self.dcnt[q][i])
        self._commit(ev, reads, writes)
        if is_output:
            self.out_events.append(ev)
        self.n_ins += 1
        return ev

    def finish(self):
        for ev in self.out_events:
            self._need("sp", ev)
        for q in self.dsem:
            for i in range(self.NDS):
                if self.dcnt[q][i] > 0:
                    self._need("sp", (self.dsem[q][i], self.dcnt[q][i]))


class _View:
    def __init__(self, t, k):
        self.t = t
        self.k = k


class Buf:
    __slots__ = ("t", "k")

    def __init__(self, t):
        self.t = t
        self.k = Tok()


def _consts():
    bf = ml_dtypes.bfloat16
    c = {}
    c["c_ident"] = np.eye(128, dtype=np.float32).astype(bf)
    c["c_ones"] = np.ones((128, 128), np.float32).astype(bf)
    rm = np.zeros((128, 128), np.float32)
    for dp in range(64):
        rm[dp + 64, dp] = -1.0
        rm[dp, dp + 64] = 1.0
    c["c_rm"] = rm
    k = np.arange(128)[:, None]
    q = np.arange(128)[None, :]
    c["c_tri"] = (k <= q).astype(np.float32).astype(bf)
    c["c_low"] = (k > q).astype(np.float32).astype(bf)
    n = np.arange(128)[:, None]
    tq = np.arange(T)[None, :]
    mc = ((16 * n + 31) <= tq) & (n < NCMP)
    c["c_maskc"] = mc.astype(np.float32).astype(bf)
    E = np.zeros((32, 16, 128), np.float32)
    for kt in range(16):
        for kk in range(128):
            E[2 * kt + (1 if kk >= 64 else 0), kt, kk] = 1.0
    c["c_E"] = E.reshape(32, 16 * 128).astype(bf)
    cs = 16 * np.arange(NCMP)[:, None]
    ss = 64 * np.arange(32)[None, :]
    ov = np.minimum(cs + 32, ss + 64) - np.maximum(cs, ss)
    mp = np.zeros((128, 32), np.float32)
    mp[:NCMP] = np.clip(ov, 0, None) / 16
    c["c_map"] = mp.astype(bf)
    tt = np.arange(T)
    tblk = (tt // 64)[:, None]
    sid = np.arange(32)[None, :]
    valid = sid <= tblk
    forced = (sid == 0) | (sid == tblk) | (sid == tblk - 1)
    vu = (valid & ~forced).astype(np.float32)
    addc = np.where(forced, BIGV * (1.0 + sid / 64.0), np.where(valid, 0.0, -BIGV * (1.0 + sid / 64.0))).astype(np.float32)
    def tm(a):
        return np.ascontiguousarray(a.reshape(16, 128, 32).transpose(1, 0, 2).reshape(128, 16 * 32)).astype(np.float32)
    c["c_vu"] = tm(vu)
    c["c_addc"] = tm(addc)
    c["c_valid"] = tm(valid.astype(np.float32))
    half = 64
    inv = (10000.0 ** (-np.arange(half, dtype=np.float32) / half)).astype(np.float32)
    c["c_inv"] = np.concatenate([inv, inv]).reshape(128, 1).astype(np.float32)
    return c


CONST_SPECS = None


def build_program(depth=4, stop=None, dbg=False):
    nc = bass.Bass("TRN2", target_bir_lowering=False)

    def din(name, shape, dt=F32):
        return nc.dram_tensor(name, list(shape), dt, kind="ExternalInput").ap()

    def dscr(name, shape, dt):
        return nc.dram_tensor(name, list(shape), dt, kind="Internal").ap()

    x_in = din("x", [T, D])
    c_in = din("c", [D])
    pos_in = din("positions", [T], I32)
    ada_w = din("ada_w", [4, D, 6 * D])
    ada_b = din("ada_b", [4, 6 * D])
    attn_norm_g = din("attn_norm_g", [4, D])
    mlp_norm_g = din("mlp_norm_g", [4, D])
    mlp_w1 = din("mlp_w1", [4, D, DFF])
    mlp_w2 = din("mlp_w2", [4, DFF, D])
    nsa_w_in = din("nsa_w_in", [2, D, 5168])
    nsa_w_out = din("nsa_w_out", [2, D, D])
    nsa_q_norm = din("nsa_q_norm", [2, 128])
    nsa_k_norm = din("nsa_k_norm", [2, 3, 128])
    nsa_cmp_pos = din("nsa_cmp_pos", [2, 2, 32, 128])
    nsa_cmp_w1 = din("nsa_cmp_w1", [2, 2, 32, 128, 256])
    nsa_cmp_w2 = din("nsa_cmp_w2", [2, 2, 256, 128])
    diff_w_in = din("diff_w_in", [2, D, 6144])
    diff_w_out = din("diff_w_out", [2, D, D])
    diff_q_norm = din("diff_q_norm", [2, 128])
    diff_k_norm = din("diff_k_norm", [2, 128])
    diff_lambda = din("diff_lambda", [2, 4, 128])
    diff_sub_norm = din("diff_sub_norm", [2, 256])
    cst = {}
    for name, arr in _consts().items():
        cst[name] = din(name, arr.shape, BF16 if arr.dtype == ml_dtypes.bfloat16 else F32)

    out = nc.dram_tensor("out", [T, D], F32, kind="ExternalOutput").ap()
    xres = dscr("xres", [T, D], F32)
    modrow = dscr("modrow", [4, 6 * D], F32)
    qT_s = dscr("qT_s", [16, 128, T], BF16)
    kT_s = dscr("kT_s", [16, 128, T], BF16)
    v_s = dscr("v_s", [T, 2048], BF16)
    dbg_out = {}
    if dbg:
        for nm, shp, dt in (("dbg_h", [128, 16 * T], BF16), ("dbg_q", [16, 128, T], BF16), ("dbg_k", [16, 128, T], BF16),
                            ("dbg_v", [T, 2048], BF16), ("dbg_x", [T, D], F32), ("dbg_m", [4, 6 * D], F32)):
            dbg_out[nm] = nc.dram_tensor(nm, shp, dt, kind="ExternalOutput").ap()

    with contextlib.ExitStack() as st:
        fw = FW(nc, st)

        def sb(name, shape, dt):
            return Buf(st.enter_context(nc.sbuf_tensor(name, list(shape), dt)))

        def ps(name, shape, dt):
            return Buf(st.enter_context(nc.psum_tensor(name, list(shape), dt)))

        hT = sb("hT", [128, KC, T], BF16)
        hT_k = [Tok() for _ in range(KC)]
        NW = 8
        wb = [sb("wb%d" % i, [128, 2048], BF16) for i in range(NW)]
        wrr = [0]
        big = sb("big", [128, 32 * 512], BF16)
        xt = [_View(big.t[:, 0:4096].bitcast(F32), Tok()), _View(big.t[:, 4096:8192].bitcast(F32), Tok())]
        xn = [_View(big.t[:, 8192 + 2048 * j_:8192 + 2048 * (j_ + 1)], Tok()) for j_ in range(4)]
        fz = sb("fz", [128, 1], F32)

        def fence():
            fw.op("dve", lambda e: e.memset(fz.t[:], 0.0), writes=[fz.k, big.k, xt[0].k, xt[1].k] + [v_.k for v_ in xn])
        gbcs = [sb("gbc%d" % i, [128, 512], F32) for i in range(2)]
        gbi = [0]
        cosT = sb("cosT", [128, T], BF16)
        sinT = sb("sinT", [128, T], BF16)
        stg = xn[0:2]
        f512 = [sb("f512_%d" % i, [128, 512], F32) for i in range(8)]
        b512 = [sb("b512_%d" % i, [128, 512], BF16) for i in range(6)]
        xs = [sb("xs%d" % i, [128, 512], F32) for i in range(8)]
        small = sb("small", [128, 64], F32)
        colm = sb("colm", [128, 8, 16], F32)
        colk = Tok()
        ident = sb("ident", [128, 128], BF16)
        ones = sb("ones", [128, 128], BF16)
        rm32 = sb("rm32", [128, 128], F32)
        rg = [sb("rg%d" % i, [128, 128], BF16) for i in range(3)]
        gcol = sb("gcol", [128, 8], F32)
        tri = sb("tri", [128, 128], BF16)
        low = sb("low", [128, 128], BF16)
        maskc = sb("maskc", [128, T], BF16)
        Emat = sb("Emat", [32, 16 * 128], BF16)
        cmap = sb("cmap", [128, 32], BF16)
        vu = sb("vu", [128, 16 * 32], F32)
        addc = sb("addc", [128, 16 * 32], F32)
        validc = sb("validc", [128, 16 * 32], F32)
        invc = sb("invc", [128, 1], F32)
        gateT = sb("gateT", [48, T], BF16)
        kcT = sb("kcT", [128, 4, 128], BF16)
        vcm = sb("vcm", [128, 4, 128], BF16)
        selw = sb("selw", [128, 160], F32)
        selb = sb("selb", [128, 32], BF16)
        selT = sb("selT", [32, 128], BF16)
        selT2 = sb("selT2", [32, 128], BF16)
        msk = [sb("msk%d" % i, [128, 128], BF16) for i in range(4)]
        lamc = sb("lamc", [128, 8], F32)
        epsc = sb("epsc", [128, 1], F32)
        fw.op("dve", lambda e: e.memset(epsc.t[:], EPS), writes=[epsc.k])
        tinyc = sb("tinyc", [128, 1], F32)
        fw.op("dve", lambda e: e.memset(tinyc.t[:], 1e-30), writes=[tinyc.k])
        fw.op("act", lambda e: e.activation(out=lamc.t[:, 6:7], in_=tinyc.t[:, 0:1], func=AF.Copy), reads=[tinyc.k], writes=[lamc.k])

        P = [ps("P%d" % i, [128, 512], F32) for i in range(7)]
        PT = ps("PT", [128, 1024], BF16)
        _ptk = Tok()
        PTk = [_ptk, _ptk]

        xk = [[Tok() for _ in range(4)] for _ in range(NT)]
        modk = [Tok() for _ in range(4)]
        qk = [Tok() for _ in range(16)]
        kk = [Tok() for _ in range(16)]
        vk = [Tok() for _ in range(4)]

        fw.op("act", lambda e: e.activation(out=lamc.t[:, 7:8], in_=epsc.t[:, 0:1], func=AF.Copy), reads=[epsc.k], writes=[lamc.k])
        def mm(out_ap, lhsT, rhs, start, stop, reads, writes):
            return fw.op("pe", lambda e: e.matmul(out_ap, lhsT=lhsT, rhs=rhs, start=start, stop=stop),
                         reads=reads, writes=writes)

        def next_wb():
            i = wrr[0]
            wrr[0] = (i + 1) % NW
            return wb[i]

        class WStream:
            def __init__(self, loads, pf=3):
                self.loads = loads
                self.pf = pf
                self.bufs = []

            def get(self, i):
                while len(self.bufs) < min(len(self.loads), i + 1 + self.pf):
                    b = next_wb()
                    self.loads[len(self.bufs)](b)
                    self.bufs.append(b)
                return self.bufs[i]

        def load_const(buf, src):
            fw.dma("sp", buf.t[:], src, writes=[buf.k])

        load_const(ident, cst["c_ident"])
        load_const(ones, cst["c_ones"])
        load_const(rm32, cst["c_rm"])
        load_const(tri, cst["c_tri"])
        load_const(low, cst["c_low"])
        load_const(maskc, cst["c_maskc"])
        load_const(Emat, cst["c_E"])
        load_const(cmap, cst["c_map"])
        load_const(vu, cst["c_vu"])
        load_const(addc, cst["c_addc"])
        load_const(validc, cst["c_valid"])
        load_const(invc, cst["c_inv"])

        rmb = sb("rmb", [128, 128], BF16)
        fw.op("dve", lambda e: e.tensor_copy(out=rmb.t[:], in_=rm32.t[:]), reads=[rm32.k], writes=[rmb.k])
        def build_rope():
            ang = xt[0]
            tmp = xt[1]
            posi_t = xn[0].t[:].bitcast(I32)
            posk = xn[0].k
            for hf in range(2):
                cs = slice(hf * 1024, (hf + 1) * 1024)
                fw.dma("sp", posi_t, pos_in[hf * 1024:(hf + 1) * 1024].partition_broadcast(128), writes=[posk])
                fw.op("dve", lambda e: e.tensor_copy(out=ang.t[:, cs], in_=posi_t), reads=[posk], writes=[ang.k])
                fw.op("dve", lambda e: e.tensor_scalar(out=ang.t[:, cs], in0=ang.t[:, cs], scalar1=invc.t[:, 0:1], scalar2=None, op0=ALU.mult),
                      reads=[ang.k, invc.k], writes=[ang.k])
                for which, dst in ((0, sinT), (1, cosT)):
                    sh = 0.0 if which == 0 else math.pi / 2
                    fw.op("dve", lambda e: e.tensor_scalar(out=tmp.t[:, cs], in0=ang.t[:, cs], scalar1=sh, scalar2=1.0 / (2 * math.pi), op0=ALU.add, op1=ALU.mult),
                          reads=[ang.k], writes=[tmp.k])
                    fw.op("dve", lambda e: e.tensor_copy(out=posi_t, in_=tmp.t[:, cs]), reads=[tmp.k], writes=[posk])
                    fw.op("dve", lambda e: e.tensor_copy(out=tmp.t[:, cs], in_=posi_t), reads=[posk], writes=[tmp.k])
                    fw.op("dve", lambda e: e.scalar_tensor_tensor(out=tmp.t[:, cs], in0=tmp.t[:, cs], scalar=-2 * math.pi, in1=ang.t[:, cs], op0=ALU.mult, op1=ALU.add),
                          reads=[tmp.k, ang.k], writes=[tmp.k])
                    fw.op("dve", lambda e: e.tensor_scalar(out=tmp.t[:, cs], in0=tmp.t[:, cs], scalar1=sh, scalar2=None, op0=ALU.add), reads=[tmp.k], writes=[tmp.k])
                    fw.op("dve", lambda e: e.tensor_scalar(out=tmp.t[:, cs], in0=tmp.t[:, cs], scalar1=-math.pi, scalar2=math.pi, op0=ALU.max, op1=ALU.min),
                          reads=[tmp.k], writes=[tmp.k])
                    fw.op("act", lambda e: e.activation(out=dst.t[:, cs], in_=tmp.t[:, cs], func=AF.Sin), reads=[tmp.k], writes=[dst.k])

        build_rope()

        for tt in range(NT):
            b = xt[tt % 2]
            fw.dma("sp", b.t[:], x_in[tt * 128:(tt + 1) * 128, :], writes=[b.k])
            fw.dma("sp", xres[tt * 128:(tt + 1) * 128, :], b.t[:], reads=[b.k], writes=xk[tt])

        def phase_ada(nl):
            ccol = small
            cb = sb("cb", [128, 16], BF16)
            fw.dma("sp", ccol.t[:, 0:16], c_in.rearrange("(kc p) -> p kc", p=128), writes=[ccol.k], allow_slow_non_contiguous=True)
            fw.op("act", lambda e: e.activation(out=cb.t[:], in_=ccol.t[:, 0:16], func=AF.Silu), reads=[ccol.k], writes=[cb.k])
            class _V:
                pass
            brow = _V(); brow.t = xt[1].t[0:1, :]; brow.k = xt[1].k
            mrow = _V(); mrow.t = xt[0].t[0:1, :]; mrow.k = xt[0].k
            for l in range(nl):
                for eg in range(6):
                    c0 = eg * 2048
                    loads = []
                    for kc in range(KC):
                        def ld(b, kc=kc):
                            fw.dma("pool", b.t[:], ada_w[l, kc * 128:(kc + 1) * 128, c0:c0 + 2048], writes=[b.k])
                        loads.append(ld)
                    ws = WStream(loads)
                    for kc in range(KC):
                        b = ws.get(kc)
                        for j in range(4):
                            mm(P[j].t[0:1, :], cb.t[:, kc:kc + 1], b.t[:, j * 512:(j + 1) * 512], kc == 0, kc == KC - 1,
                               [cb.k, b.k], [P[j].k])
                    fw.dma("sp", brow.t, ada_b[l:l + 1, c0:c0 + 2048], writes=[brow.k])
                    for j in range(4):
                        fw.op("dve", lambda e: e.tensor_tensor(out=mrow.t[:, j * 512:(j + 1) * 512], in0=P[j].t[0:1, :],
                                                               in1=brow.t[:, j * 512:(j + 1) * 512], op=ALU.add),
                              reads=[P[j].k, brow.k], writes=[mrow.k])
                    fw.dma("sp", modrow[l:l + 1, c0:c0 + 2048], mrow.t, reads=[mrow.k], writes=[modk[l]])

        phase_ada(depth)
        if dbg:
            for l in range(depth):
                fw.dma("sp", small.t[0:1, 0:1], modrow[l:l + 1, 0:1], reads=[modk[l]], writes=[small.k])
            for l in range(depth):
                fw.dma("sp", dbg_out["dbg_m"][l:l + 1, :], modrow[l:l + 1, :], reads=[modk[l], small.k], is_output=True)

        def load_layer_cols(l):
            def col(dst_j, src_ap):
                fw.dma("sp", colm.t[:, dst_j, :], src_ap.rearrange("(kc p) -> p kc", p=128), reads=[modk[l]], writes=[colk],
                       allow_slow_non_contiguous=True)
            col(0, modrow[l, 0:D])
            col(1, modrow[l, D:2 * D])
            col(2, modrow[l, 3 * D:4 * D])
            col(3, modrow[l, 4 * D:5 * D])
            col(4, attn_norm_g[l, :])
            col(5, mlp_norm_g[l, :])
            for a, scj, gj in ((6, 1, 4), (7, 3, 5)):
                fw.op("dve", lambda e: e.scalar_tensor_tensor(out=colm.t[:, a, :], in0=colm.t[:, scj, :], scalar=1.0, in1=colm.t[:, gj, :],
                                                              op0=ALU.add, op1=ALU.mult), reads=[colk], writes=[colk])

        def load_gbc(l, which, dt):
            off = (2 * D if which == 0 else 5 * D) + dt * 512
            gbi[0] += 1
            g_ = gbcs[gbi[0] % 2]
            fw.dma("sp", g_.t[:], modrow[l, off:off + 512].partition_broadcast(128), reads=[modk[l]], writes=[g_.k])
            return g_

        def phase_norm(l, which):
            aj, shj = (6, 0) if which == 0 else (7, 2)
            fence()
            P6b = P[6].t[:].bitcast(BF16)
            banks = [(PT.t, PTk[0]), (P6b, P[6].k)]
            for tg in range(4):
                for j in range(4):
                    tt = tg * 4 + j
                    xb_ = xt[tt % 2]
                    xnb = xn[j]
                    fw.dma("sp", xb_.t[:], xres[tt * 128:(tt + 1) * 128, :], reads=xk[tt], writes=[xb_.k])
                    ssq = small.t[:, 32 + (tt % 2) * 4: 33 + (tt % 2) * 4]
                    rs = small.t[:, 34 + (tt % 2) * 4: 35 + (tt % 2) * 4]
                    fw.op("dve", lambda e: e.memset(ssq, 0.0), writes=[small.k])
                    fw.op("act", lambda e: e.activation(out=xnb.t[:], in_=xb_.t[:], func=AF.Square, accum_out=ssq),
                          reads=[xb_.k, small.k], writes=[xnb.k, small.k])
                    fw.op("act", lambda e: e.activation(out=rs, in_=ssq, func=AF.Sqrt, scale=1.0 / D, bias=epsc.t[:, 0:1]), reads=[small.k], writes=[small.k])
                    fw.op("dve", lambda e: e.reciprocal(out=rs, in_=rs), reads=[small.k], writes=[small.k])
                    fw.op("dve", lambda e: e.tensor_scalar(out=xnb.t[:], in0=xb_.t[:], scalar1=rs, scalar2=None, op0=ALU.mult),
                          reads=[xb_.k, small.k], writes=[xnb.k])
                for kc in range(KC):
                    bt, bk = banks[kc % 2]
                    for j in range(4):
                        fw.op("pe", lambda e: e.transpose(out=bt[:, j * 128:(j + 1) * 128], in_=xn[j].t[:, kc * 128:(kc + 1) * 128], identity=ident.t[:]),
                              reads=[xn[j].k, ident.k], writes=[bk])
                    fw.op("act", lambda e: e.activation(out=hT.t[:, kc, tg * 512:(tg + 1) * 512], in_=bt[:, 0:512], func=AF.Identity,
                                                        bias=colm.t[:, shj, kc:kc + 1], scale=colm.t[:, aj, kc:kc + 1]),
                          reads=[bk, colk], writes=[hT_k[kc]])

        def normrope(pj, n, gci, rgb, cos_ap, sin_ap, out_ap, out_reads, out_writes, wi):
            xb = b512[(2 * wi) % 6]
            sq = b512[(2 * wi + 1) % 6]
            t1 = f512[(3 * wi) % 6]
            t2 = f512[(3 * wi + 1) % 6]
            rsb = f512[(3 * wi + 2) % 6]
            prot = P[2 + (wi % 2)]
            pssq = P[4 + (wi % 2)]
            fw.op("act", lambda e: e.activation(out=xb.t[:, 0:n], in_=pj.t[:, 0:n], func=AF.Identity, scale=gcol.t[:, gci:gci + 1]),
                  reads=[pj.k, gcol.k], writes=[xb.k])
            fw.op("act", lambda e: e.activation(out=sq.t[:, 0:n], in_=pj.t[:, 0:n], func=AF.Square), reads=[pj.k], writes=[sq.k])

            def part2():
                mm(prot.t[:, 0:n], rmb.t[:], xb.t[:, 0:n], True, True, [rmb.k, xb.k], [prot.k])
                mm(pssq.t[:, 0:n], ones.t[:], sq.t[:, 0:n], True, True, [ones.k, sq.k], [pssq.k])
                fw.op("dve", lambda e: e.tensor_tensor(out=t1.t[:, 0:n], in0=xb.t[:, 0:n], in1=cos_ap, op=ALU.mult), reads=[xb.k, cosT.k], writes=[t1.k])
                fw.op("dve", lambda e: e.tensor_tensor(out=t2.t[:, 0:n], in0=prot.t[:, 0:n], in1=sin_ap, op=ALU.mult),
                      reads=[prot.k, sinT.k], writes=[t2.k])
                fw.op("act", lambda e: e.activation(out=rsb.t[:, 0:n], in_=pssq.t[:, 0:n], func=AF.Ln, scale=1.0 / 128, bias=epsc.t[:, 0:1]),
                      reads=[pssq.k], writes=[rsb.k])
                fw.op("act", lambda e: e.activation(out=rsb.t[:, 0:n], in_=rsb.t[:, 0:n], func=AF.Exp, scale=-0.5), reads=[rsb.k], writes=[rsb.k])
                fw.op("pool", lambda e: e.tensor_tensor(out=t1.t[:, 0:n], in0=t1.t[:, 0:n], in1=t2.t[:, 0:n], op=ALU.add), reads=[t1.k, t2.k], writes=[t1.k])
                fw.op("dve", lambda e: e.tensor_tensor(out=out_ap, in0=t1.t[:, 0:n], in1=rsb.t[:, 0:n], op=ALU.mult),
                      reads=[t1.k, rsb.k] + out_reads, writes=out_writes)
            return part2

        def make_rg(i, gci):
            return

        def load_gcol(j, src_ap):
            fw.dma("sp", gcol.t[:, j:j + 1], src_ap.rearrange("(p o) -> p o", o=1), writes=[gcol.k])

        def proj_fm(W3, col0, ws, wi, epilogue):
            b = ws.get(wi)
            bv = b.t[:].rearrange("p (kc n) -> p kc n", kc=KC)
            for tb in range(4):
                pj = P[tb % 2]
                for kc in range(KC):
                    mm(pj.t[:], bv[:, kc, :], hT.t[:, kc, tb * 512:(tb + 1) * 512], kc == 0, kc == KC - 1,
                       [b.k, hT_k[kc]], [pj.k])
                d_ = epilogue(tb, pj)
                flush_pending()
                if d_ is not None:
                    pending.append(d_)

        pending = []

        def flush_pending():
            for f_ in pending:
                f_()
            del pending[:]

        def fm_loads(W3, cols):
            loads = []
            for col0 in cols:
                def ld(b, col0=col0):
                    fw.dma("pool", b.t[:].rearrange("p (kc n) -> p kc n", kc=KC), W3[:, :, col0:col0 + 128], writes=[b.k])
                loads.append(ld)
            return loads

        def proj_tm(W3, col0, ncols, epilogue):
            loads = []
            for g4 in range(4):
                def ld(b, g4=g4):
                    fw.dma("pool", b.t[:, 0:4 * ncols].rearrange("p (kc n) -> p kc n", kc=4), W3[:, g4 * 4:(g4 + 1) * 4, col0:col0 + ncols], writes=[b.k])
                loads.append(ld)
            ws = WStream(loads, pf=3)
            bufs = [ws.get(i) for i in range(4)]
            for tt in range(NT):
                pj = P[tt % 2]
                for kc in range(KC):
                    b = bufs[kc // 4]
                    bv = b.t[:, 0:4 * ncols].rearrange("p (kc n) -> p kc n", kc=4)
                    mm(pj.t[:, 0:ncols], hT.t[:, kc, tt * 128:(tt + 1) * 128], bv[:, kc % 4, :], kc == 0, kc == KC - 1,
                       [b.k, hT_k[kc]], [pj.k])
                epilogue(tt, pj)

        def resid_load(tt, dt, slot):
            xsb = xs[slot % 8]
            fw.dma("sp", xsb.t[:], xres[tt * 128:(tt + 1) * 128, dt * 512:(dt + 1) * 512], reads=[xk[tt][dt]], writes=[xsb.k])

        def resid_mult(pj, g_, slot):
            tmp = f512[slot % 4]
            fw.op("dve", lambda e: e.tensor_tensor(out=tmp.t[:], in0=pj.t[:], in1=g_.t[:], op=ALU.mult),
                  reads=[pj.k, g_.k], writes=[tmp.k])

        def resid_store(tt, dt, slot, dst=None):
            xsb = xs[slot % 8]
            tmp = f512[slot % 4]
            fw.op("dve", lambda e: e.tensor_tensor(out=xsb.t[:], in0=xsb.t[:], in1=tmp.t[:], op=ALU.add), reads=[xsb.k, tmp.k], writes=[xsb.k])
            fw.dma("sp", xres[tt * 128:(tt + 1) * 128, dt * 512:(dt + 1) * 512], xsb.t[:], reads=[xsb.k], writes=[xk[tt][dt]])
            if dst is not None:
                fw.dma("sp", dst[tt * 128:(tt + 1) * 128, dt * 512:(dt + 1) * 512], xsb.t[:], reads=[xsb.k], is_output=True)

        def phase_outproj(l, Wout):
            W3 = Wout.rearrange("(kc p) n -> p kc n", p=128)
            slot = [0]
            for dt in range(4):
                g_ = load_gbc(l, 0, dt)
                resid_load(0, dt, slot[0])
                resid_load(1, dt, slot[0] + 1)

                def epi(tt, pj, dt=dt, g_=g_):
                    sl = slot[0]
                    if tt + 2 < NT:
                        resid_load(tt + 2, dt, sl + 2)
                    resid_mult(pj, g_, sl)
                    resid_store(tt, dt, sl)
                    slot[0] += 1
                proj_tm(W3, dt * 512, 512, epi)

        def phase_mlp(l, final):
            W1 = mlp_w1[l].rearrange("(kc p) n -> p kc n", p=128)
            W2 = mlp_w2[l].rearrange("(fc p) n -> p fc n", p=128)
            hid = big.t[:].rearrange("p (fc n) -> p fc n", fc=16)
            fence()
            loads = []
            for tb2 in range(2):
                for part in range(4):
                    for g4 in range(4):
                        c0 = (part * 16 + g4 * 4) * 128
                        for q in range(4):
                            def ld(b, q=q, c0=c0):
                                fw.dma("pool", b.t[:].rearrange("p (kc n) -> p kc n", kc=4), W1[:, 4 * q:4 * q + 4, c0:c0 + 512], writes=[b.k])
                            loads.append(ld)
                    for dt in range(4):
                        for q in range(4):
                            def ld(b, q=q, dt=dt, part=part):
                                fw.dma("pool", b.t[:].rearrange("p (fc n) -> p fc n", fc=4),
                                       W2[:, part * 16 + 4 * q: part * 16 + 4 * q + 4, dt * 512:(dt + 1) * 512], writes=[b.k])
                            loads.append(ld)
            ws = WStream(loads, pf=4)
            wi = 0
            cnt = 0
            slot = 0
            for tb2 in range(2):
                tok0 = tb2 * 1024
                for part in range(4):
                    for g4 in range(4):
                        tl = [ws.get(wi + q) for q in range(4)]
                        wi += 4
                        tv = [b.t[:].rearrange("p (kc n) -> p kc n", kc=4) for b in tl]
                        for fcl in range(4):
                            fc = g4 * 4 + fcl
                            for tbb in range(2):
                                pj = P[cnt % 2]
                                r = b512[cnt % 6]
                                cnt += 1
                                for kc in range(KC):
                                    mm(pj.t[:], tv[kc // 4][:, kc % 4, fcl * 128:(fcl + 1) * 128],
                                       hT.t[:, kc, tok0 + tbb * 512: tok0 + (tbb + 1) * 512], kc == 0, kc == KC - 1,
                                       [tl[kc // 4].k, hT_k[kc]], [pj.k])
                                fw.op("act", lambda e: e.activation(out=r.t[:], in_=pj.t[:], func=AF.Relu), reads=[pj.k], writes=[r.k])
                                fw.op("dve", lambda e: e.tensor_tensor(out=hid[:, fc, tbb * 512:(tbb + 1) * 512], in0=r.t[:], in1=r.t[:], op=ALU.mult),
                                      reads=[r.k], writes=[big.k])
                    for dt in range(4):
                        g_ = load_gbc(l, 1, dt)
                        tl = [ws.get(wi + q) for q in range(4)]
                        wi += 4
                        tv = [b.t[:].rearrange("p (fc n) -> p fc n", fc=4) for b in tl]
                        for ps_ in range(2):
                            for t4 in range(4):
                                resid_load(tb2 * 8 + ps_ * 4 + t4, dt, slot + ps_ * 4 + t4)
                        for ps_ in range(2):
                            for fc in range(16):
                                for t4 in range(4):
                                    tl_ = ps_ * 4 + t4
                                    mm(P[2 + t4].t[:], hid[:, fc, tl_ * 128:(tl_ + 1) * 128], tv[fc // 4][:, fc % 4, :], fc == 0, fc == 15,
                                       [big.k, tl[fc // 4].k], [P[2 + t4].k])
                            for t4 in range(4):
                                resid_mult(P[2 + t4], g_, slot + ps_ * 4 + t4)
                            for t4 in range(4):
                                resid_store(tb2 * 8 + ps_ * 4 + t4, dt, slot + ps_ * 4 + t4, dst=(out if (final and part == 3) else None))
                        slot += 8

        def phase_nsa(l, j):
            Win = nsa_w_in[j].rearrange("(kc p) n -> p kc n", p=128)
            load_gcol(0, nsa_q_norm[j, :])
            for i in range(3):
                load_gcol(1 + i, nsa_k_norm[j, i, :])
            make_rg(0, 0)
            make_rg(1, 2)
            make_rg(2, 3)
            chunks = []
            for h in range(16):
                chunks.append((h * 128, "nr", qT_s, qk, h, 0, 0))
            for g in range(4):
                chunks.append((2048 + g * 128, "raw", kT_s, kk, g, None, None))
            for g in range(4):
                chunks.append((2560 + g * 128, "raw", kT_s, kk, 4 + g, None, None))
            for g in range(4):
                chunks.append((3072 + g * 128, "nr", kT_s, kk, 8 + g, 2, 1))
            for g in range(4):
                chunks.append((4096 + g * 128, "nr", kT_s, kk, 12 + g, 3, 2))
            ws = WStream(fm_loads(Win, [c[0] for c in chunks]), pf=3)
            cnt = [0]
            for ci, (col0, kind, dst, dk, idx, gci, rgi) in enumerate(chunks):
                sg = stg[ci % 2]

                def epi(tb, pj, kind=kind, gci=gci, rgi=rgi, sg=sg):
                    o_ap = sg.t[:, tb * 512:(tb + 1) * 512]
                    if kind == "raw":
                        fw.op("act", lambda e: e.activation(out=o_ap, in_=pj.t[:], func=AF.Copy), reads=[pj.k], writes=[sg.k])
                        return None
                    else:
                        p2 = normrope(pj, 512, gci, rg[rgi], cosT.t[:, tb * 512:(tb + 1) * 512], sinT.t[:, tb * 512:(tb + 1) * 512],
                                      o_ap, [], [sg.k], cnt[0])
                        cnt[0] += 1
                        return p2
                proj_fm(Win, col0, ws, ci, epi)
                pending.append(lambda dst=dst, idx=idx, sg=sg, dk=dk: fw.dma("sp", dst[idx], sg.t[:], reads=[sg.k], writes=[dk[idx]]))
            flush_pending()
            for vi, col0 in enumerate((3584, 4608)):
                def epi(tt, pj, vi=vi):
                    o = b512[tt % 6]
                    fw.op("act", lambda e: e.activation(out=o.t[:], in_=pj.t[:], func=AF.Copy), reads=[pj.k], writes=[o.k])
                    fw.dma("sp", v_s[tt * 128:(tt + 1) * 128, vi * 512:(vi + 1) * 512], o.t[:], reads=[o.k], writes=[vk[vi]])
                proj_tm(Win, col0, 512, epi)
            gb = next_wb()
            fw.dma("pool", gb.t[:, 0:KC * 48].rearrange("p (kc n) -> p kc n", kc=KC), Win[:, :, 5120:5168], writes=[gb.k])
            gv = gb.t[:, 0:KC * 48].rearrange("p (kc n) -> p kc n", kc=KC)
            for tb in range(4):
                pj = P[tb % 2]
                for kc in range(KC):
                    mm(pj.t[0:48, :], gv[:, kc, :], hT.t[:, kc, tb * 512:(tb + 1) * 512], kc == 0, kc == KC - 1, [gb.k, hT_k[kc]], [pj.k])
                fw.op("act", lambda e: e.activation(out=gateT.t[:, tb * 512:(tb + 1) * 512], in_=pj.t[0:48, :], func=AF.Sigmoid),
                      reads=[pj.k], writes=[gateT.k])
            if stop == "nsa_proj":
                return
            fence()
            kall = big.t[:, 0:4 * T].rearrange("p (g t) -> p g t", g=4)
            hidc = big.t[:, 4 * T:4 * T + 2 * 512].rearrange("p (fc n) -> p fc n", fc=2)
            posT = small.t[:, 0:32]
            posTb = b512[0]
            fw.op("dve", lambda e: e.memset(kcT.t[:], 0.0), writes=[kcT.k])
            fw.op("dve", lambda e: e.memset(vcm.t[:], 0.0), writes=[vcm.k])
            for kv in range(2):
                for g in range(4):
                    fw.dma("sp", kall[:, g, :], kT_s[kv * 4 + g], reads=[kk[kv * 4 + g]], writes=[big.k])
                fw.dma("sp", posT, nsa_cmp_pos[j, kv].rearrange("l d -> d l"), writes=[small.k], allow_slow_non_contiguous=True)
                fw.op("dve", lambda e: e.tensor_copy(out=posTb.t[:, 0:32], in_=posT), reads=[small.k], writes=[posTb.k])
                w1v = nsa_cmp_w1[j, kv].rearrange("l d f -> d l f")
                w1b = []
                for q4 in range(4):
                    b = next_wb()
                    fw.dma("pool", b.t[:].rearrange("p (l f) -> p l f", l=8), w1v[:, q4 * 8:(q4 + 1) * 8, :], writes=[b.k])
                    w1b.append(b)
                w2b = next_wb()
                fw.dma("pool", w2b.t[:, 0:256].rearrange("p (fc d) -> p fc d", fc=2), nsa_cmp_w2[j, kv].rearrange("(fc p) d -> p fc d", p=128), writes=[w2b.k])
                w2v = w2b.t[:, 0:256].rearrange("p (fc d) -> p fc d", fc=2)
                biasc = small.t[:, 40:42]
                for fc in range(2):
                    pb_ = P[4]
                    for lq in range(32):
                        bv = w1b[lq // 8].t[:].rearrange("p (l f) -> p l f", l=8)
                        mm(pb_.t[:, 0:1], bv[:, lq % 8, fc * 128:(fc + 1) * 128], posTb.t[:, lq:lq + 1], lq == 0, lq == 31,
                           [w1b[lq // 8].k, posTb.k], [pb_.k])
                    fw.op("dve", lambda e: e.tensor_copy(out=small.t[:, 40 + fc:41 + fc], in_=pb_.t[:, 0:1]), reads=[pb_.k], writes=[small.k])
                    ph = P[fc]
                    for lq in range(32):
                        bv = w1b[lq // 8].t[:].rearrange("p (l f) -> p l f", l=8)
                        mm(ph.t[:, 0:508].rearrange("p (g n) -> p g n", g=4), bv[:, lq % 8, fc * 128:(fc + 1) * 128],
                           kall[:, :, lq:lq + 16 * (NCMP - 1) + 1:16], lq == 0, lq == 31, [w1b[lq // 8].k, big.k], [ph.k])
                    fw.op("act", lambda e: e.activation(out=hidc[:, fc, 0:508], in_=ph.t[:, 0:508], func=AF.Silu, bias=small.t[:, 40 + fc:41 + fc]),
                          reads=[ph.k, small.k], writes=[big.k])
                if kv == 0:
                    make_rg(1, 1)
                    pk = P[6]
                    for fc in range(2):
                        mm(pk.t[:, 0:508], w2v[:, fc, :], hidc[:, fc, 0:508], fc == 0, fc == 1, [w2b.k, big.k], [pk.k])
                    for g in range(4):
                        v = _View(pk.t[:, g * NCMP:(g + 1) * NCMP], pk.k)
                        normrope(v, NCMP, 1, rg[1], cosT.t[:, 31:31 + 16 * (NCMP - 1) + 1:16], sinT.t[:, 31:31 + 16 * (NCMP - 1) + 1:16],
                                 kcT.t[:, g, 0:NCMP], [], [kcT.k], g)()
                    make_rg(1, 2)
                else:
                    for g in range(4):
                        pv = P[6]
                        for fc in range(2):
                            mm(pv.t[0:NCMP, 0:128], hidc[:, fc, g * NCMP:(g + 1) * NCMP], w2v[:, fc, :], fc == 0, fc == 1, [big.k, w2b.k], [pv.k])
                        fw.op("act", lambda e: e.activation(out=vcm.t[0:NCMP, g, :], in_=pv.t[0:NCMP, 0:128], func=AF.Copy), reads=[pv.k], writes=[vcm.k])
            if stop == "nsa_cmp":
                return
            qall = big.t[:, 0:4 * T].rearrange("p (qt r q) -> p qt r q", qt=16, r=4)
            ksT = big.t[:, 4 * T:5 * T]
            kwT = big.t[:, 5 * T:6 * T]
            vsb = big.t[:, 6 * T:7 * T].rearrange("p (tt d) -> p tt d", tt=16)
            vwb = big.t[:, 7 * T:8 * T].rearrange("p (tt d) -> p tt d", tt=16)
            vu3 = vu.t[:].rearrange("p (qt s) -> p qt s", qt=16)
            addc3 = addc.t[:].rearrange("p (qt s) -> p qt s", qt=16)
            val3 = validc.t[:].rearrange("p (qt s) -> p qt s", qt=16)
            E3 = Emat.t[:].rearrange("s (kt k) -> s kt k", kt=16)
            PTf = PT.t[:].bitcast(F32)
            Sb = [P[0], P[1], P[5]]
            LA = 3
            LE = 2
            selTs = [selT, selT2]
            r4 = lambda ap: ap.rearrange("p (r q) -> p r q", r=4)
            for g in range(4):
                for r in range(4):
                    fw.dma("sp", qall[:, :, r, :], qT_s[g * 4 + r].rearrange("p (qt q) -> p qt q", q=128), reads=[qk[g * 4 + r]], writes=[big.k])
                fw.dma("sp", ksT, kT_s[8 + g], reads=[kk[8 + g]], writes=[big.k])
                fw.dma("sp", kwT, kT_s[12 + g], reads=[kk[12 + g]], writes=[big.k])
                fw.dma("sp", vsb, v_s[:, g * 128:(g + 1) * 128].rearrange("(tt p) d -> p tt d", p=128), reads=[vk[0]], writes=[big.k])
                fw.dma("sp", vwb, v_s[:, 512 + g * 128:512 + (g + 1) * 128].rearrange("(tt p) d -> p tt d", p=128), reads=[vk[1]], writes=[big.k])
                cnt_i = [0]

                def run_items(qt, items, accb, selTb, is_A):
                    qcols = qall[:, qt, :, :]
                    n_it = len(items)
                    base = cnt_i[0]
                    cnt_i[0] += n_it

                    def S_emit(i):
                        x, kt, fst, lst = items[i]
                        S = Sb[(base + i) % 3]
                        if x == 0:
                            kap, kr = kcT.t[:, g, :], [kcT.k]
                        elif x == 1:
                            kap, kr = ksT[:, kt * 128:(kt + 1) * 128], []
                        else:
                            kap, kr = kwT[:, kt * 128:(kt + 1) * 128], []
                        mm(r4(S.t[:]), kap, qcols, True, True, kr + [big.k], [S.k])
                        if x == 1:
                            sl = ((base + i) % 4) * 128
                            mm(P[4].t[:, sl:sl + 128], E3[:, kt, :], selTb.t[:], True, True, [Emat.k, selTb.k], [P[4].k])
                            if kt == qt:
                                mb = msk[(base + i) % 4]
                                fw.op("dve", lambda e: e.tensor_tensor(out=mb.t[:], in0=P[4].t[:, sl:sl + 128], in1=tri.t[:], op=ALU.mult),
                                      reads=[P[4].k, tri.k], writes=[mb.k])

                    def exp_emit(i):
                        x, kt, fst, lst = items[i]
                        S = Sb[(base + i) % 3]
                        pT = b512[(base + i) % 4]
                        fw.op("act", lambda e: e.activation(out=pT.t[:], in_=S.t[:], func=AF.Exp, scale=SCALE), reads=[S.k], writes=[pT.k])
                        mk = None
                        if x == 0:
                            mk = (maskc.t[:, qt * 128:(qt + 1) * 128].unsqueeze(1).to_broadcast([128, 4, 128]), [maskc.k])
                        elif x == 1:
                            if kt == qt:
                                mk = (msk[(base + i) % 4].t[:].unsqueeze(1).to_broadcast([128, 4, 128]), [msk[(base + i) % 4].k])
                            else:
                                sl = ((base + i) % 4) * 128
                                mk = (P[4].t[:, sl:sl + 128].unsqueeze(1).to_broadcast([128, 4, 128]), [P[4].k])
                        elif kt == qt:
                            mk = (tri.t[:].unsqueeze(1).to_broadcast([128, 4, 128]), [tri.k])
                        elif kt == qt - 4:
                            mk = (low.t[:].unsqueeze(1).to_broadcast([128, 4, 128]), [low.k])
                        if mk is not None:
                            fw.op("dve", lambda e: e.tensor_tensor(out=r4(pT.t[:]), in0=r4(pT.t[:]), in1=mk[0], op=ALU.mult),
                                  reads=[pT.k] + mk[1], writes=[pT.k])

                    def finish_branch(x):
                        Ob, Db = P[2], P[3]
                        osb = f512[2]
                        rden = f512[3]
                        cf = f512[4]
                        fw.op("act", lambda e: e.activation(out=rden.t[:], in_=Db.t[:], func=AF.Ln, bias=tinyc.t[:, 0:1]), reads=[Db.k], writes=[rden.k])
                        fw.op("act", lambda e: e.activation(out=osb.t[:], in_=Ob.t[:], func=AF.Copy), reads=[Ob.k], writes=[osb.k])
                        fw.op("act", lambda e: e.activation(out=rden.t[:], in_=rden.t[:], func=AF.Exp, scale=-1.0), reads=[rden.k], writes=[rden.k])
                        Gb = P[6]
                        for r in range(4):
                            cidx = x * 16 + g * 4 + r
                            mm(Gb.t[:, r * 128:(r + 1) * 128], ident.t[0:48, cidx:cidx + 1].to_broadcast([48, 128]),
                               gateT.t[:, qt * 128:(qt + 1) * 128], True, True, [ident.k, gateT.k], [Gb.k])
                        fw.op("dve", lambda e: e.tensor_tensor(out=cf.t[:], in0=Gb.t[:], in1=rden.t[:], op=ALU.mult), reads=[Gb.k, rden.k], writes=[cf.k])
                        if x == 0:
                            fw.op("pool", lambda e: e.tensor_tensor(out=accb.t[:], in0=osb.t[:], in1=cf.t[:], op=ALU.mult), reads=[osb.k, cf.k], writes=[accb.k])
                        else:
                            fw.op("dve", lambda e: e.tensor_tensor(out=cf.t[:], in0=osb.t[:], in1=cf.t[:], op=ALU.mult), reads=[osb.k, cf.k], writes=[cf.k])
                            fw.op("pool", lambda e: e.tensor_tensor(out=accb.t[:], in0=accb.t[:], in1=cf.t[:], op=ALU.add), reads=[accb.k, cf.k], writes=[accb.k])
                        return rden

                    def topk_chain(pTc, rden):
                        pn = b512[4]
                        fw.op("dve", lambda e: e.tensor_tensor(out=pn.t[:], in0=pTc.t[:], in1=rden.t[:], op=ALU.mult), reads=[pTc.k, rden.k], writes=[pn.k])
                        Pi = PTf[:, 256:288]
                        for r in range(4):
                            mm(Pi, pn.t[:, r * 128:(r + 1) * 128], cmap.t[:], r == 0, r == 3, [pn.k, cmap.k], [PTk[0]])
                        iv = selw.t[:, 0:32]
                        iv2 = selw.t[:, 32:64]
                        m8a = selw.t[:, 64:72]
                        m8b = selw.t[:, 72:80]
                        fw.op("dve", lambda e: e.tensor_tensor(out=iv, in0=Pi, in1=vu3[:, qt, :], op=ALU.mult), reads=[PTk[0], vu.k], writes=[selw.k])
                        fw.op("dve", lambda e: e.tensor_tensor(out=iv, in0=iv, in1=addc3[:, qt, :], op=ALU.add), reads=[selw.k, addc.k], writes=[selw.k])
                        fw.op("dve", lambda e: e.max(out=m8a, in_=iv), reads=[selw.k], writes=[selw.k])
                        fw.op("dve", lambda e: e.match_replace(out=iv2, in_to_replace=m8a, in_values=iv, imm_value=-3.0e38), reads=[selw.k], writes=[selw.k])
                        fw.op("dve", lambda e: e.max(out=m8b, in_=iv2), reads=[selw.k], writes=[selw.k])
                        fw.op("dve", lambda e: e.tensor_scalar(out=iv2, in0=iv, scalar1=selw.t[:, 79:80], scalar2=None, op0=ALU.is_ge), reads=[selw.k], writes=[selw.k])
                        fw.op("dve", lambda e: e.tensor_tensor(out=selb.t[:], in0=iv2, in1=val3[:, qt, :], op=ALU.mult), reads=[selw.k, validc.k], writes=[selb.k])
                        fw.op("pe", lambda e: e.transpose(out=PT.t[0:32, 0:128], in_=selb.t[:], identity=ident.t[:]), reads=[selb.k, ident.k], writes=[PTk[0]])
                        fw.op("act", lambda e: e.activation(out=selTb.t[:], in_=PT.t[0:32, 0:128], func=AF.Copy), reads=[PTk[0]], writes=[selTb.k])

                    def pv_emit(i):
                        x, kt, fst, lst = items[i]
                        pT = b512[(base + i) % 4]
                        if x == 0:
                            vap, vr = vcm.t[:, g, :], [vcm.k]
                        elif x == 1:
                            vap, vr = vsb[:, kt, :], [big.k]
                        else:
                            vap, vr = vwb[:, kt, :], [big.k]
                        mm(P[2].t[:], vap, pT.t[:], fst, lst, vr + [pT.k], [P[2].k])
                        mm(P[3].t[:], ones.t[:], pT.t[:], fst, lst, [ones.k, pT.k], [P[3].k])
                        if lst:
                            rden = finish_branch(x)
                            if x == 0:
                                topk_chain(pT, rden)

                    s_em = 0
                    e_em = 0
                    for i in range(n_it):
                        while s_em < min(n_it, i + LA):
                            S_emit(s_em)
                            s_em += 1
                        while e_em < min(n_it, i + LE):
                            exp_emit(e_em)
                            e_em += 1
                        pv_emit(i)
                    if not is_A:
                        for r in range(4):
                            h = g * 4 + r
                            fw.op("act", lambda e: e.activation(out=hT.t[:, h, qt * 128:(qt + 1) * 128], in_=accb.t[:, r * 128:(r + 1) * 128], func=AF.Copy),
                                  reads=[accb.k], writes=[hT_k[h]])

                def stage_A(qt):
                    run_items(qt, [(0, 0, True, True)], f512[5 + (qt % 2)], selTs[qt % 2], True)

                def stage_B(qt):
                    items = []
                    wl = list(range(max(0, qt - 4), qt + 1))
                    for ii, kt in enumerate(wl):
                        items.append((2, kt, ii == 0, ii == len(wl) - 1))
                    for kt in range(qt + 1):
                        items.append((1, kt, kt == 0, kt == qt))
                    run_items(qt, items, f512[5 + (qt % 2)], selTs[qt % 2], False)

                stage_A(0)
                for qt in range(NT):
                    if qt + 1 < NT:
                        stage_A(qt + 1)
                    stage_B(qt)
            if stop == "nsa_attn":
                return
            phase_outproj(l, nsa_w_out[j])

        def phase_diff(l, j, lam_init):
            Win = diff_w_in[j].rearrange("(kc p) n -> p kc n", p=128)
            load_gcol(0, diff_q_norm[j, :])
            load_gcol(1, diff_k_norm[j, :])
            load_gcol(4, diff_sub_norm[j, 0:128])
            load_gcol(5, diff_sub_norm[j, 128:256])
            make_rg(0, 0)
            make_rg(1, 1)
            lb = f512[0]
            fw.dma("sp", lb.t[:], diff_lambda[j].rearrange("a d -> (a d)").partition_broadcast(128), writes=[lb.k])
            fw.op("dve", lambda e: e.tensor_tensor(out=lb.t[:, 0:128], in0=lb.t[:, 0:128], in1=lb.t[:, 128:256], op=ALU.mult), reads=[lb.k], writes=[lb.k])
            fw.op("dve", lambda e: e.tensor_tensor(out=lb.t[:, 256:384], in0=lb.t[:, 256:384], in1=lb.t[:, 384:512], op=ALU.mult), reads=[lb.k], writes=[lb.k])
            fw.op("dve", lambda e: e.reduce_sum(out=lamc.t[:, 0:1], in_=lb.t[:, 0:128], axis=AX.X), reads=[lb.k], writes=[lamc.k])
            fw.op("dve", lambda e: e.reduce_sum(out=lamc.t[:, 1:2], in_=lb.t[:, 256:384], axis=AX.X), reads=[lb.k], writes=[lamc.k])
            fw.op("act", lambda e: e.activation(out=lamc.t[:, 2:4], in_=lamc.t[:, 0:2], func=AF.Exp), reads=[lamc.k], writes=[lamc.k])
            fw.op("dve", lambda e: e.tensor_tensor(out=lamc.t[:, 4:5], in0=lamc.t[:, 3:4], in1=lamc.t[:, 2:3], op=ALU.subtract), reads=[lamc.k], writes=[lamc.k])
            fw.op("dve", lambda e: e.tensor_scalar(out=lamc.t[:, 4:5], in0=lamc.t[:, 4:5], scalar1=-lam_init, scalar2=None, op0=ALU.add), reads=[lamc.k], writes=[lamc.k])
            fw.op("dve", lambda e: e.tensor_scalar(out=gcol.t[:, 4:6], in0=gcol.t[:, 4:6], scalar1=(1.0 - lam_init), scalar2=None, op0=ALU.mult),
                  reads=[gcol.k], writes=[gcol.k])
            chunks = []
            for m in range(16):
                chunks.append((m * 128, qT_s, qk, m, 0, 0))
            for m in range(16):
                chunks.append((2048 + m * 128, kT_s, kk, m, 1, 1))
            ws = WStream(fm_loads(Win, [c[0] for c in chunks]), pf=3)
            cnt = [0]
            for ci, (col0, dst, dk, idx, gci, rgi) in enumerate(chunks):
                sg = stg[ci % 2]

                def epi(tb, pj, gci=gci, rgi=rgi, sg=sg):
                    p2 = normrope(pj, 512, gci, rg[rgi], cosT.t[:, tb * 512:(tb + 1) * 512], sinT.t[:, tb * 512:(tb + 1) * 512],
                                  sg.t[:, tb * 512:(tb + 1) * 512], [], [sg.k], cnt[0])
                    cnt[0] += 1
                    return p2
                proj_fm(Win, col0, ws, ci, epi)
                pending.append(lambda dst=dst, idx=idx, sg=sg, dk=dk: fw.dma("sp", dst[idx], sg.t[:], reads=[sg.k], writes=[dk[idx]]))
            flush_pending()
            for vi in range(4):
                def epi(tt, pj, vi=vi):
                    o = b512[tt % 6]
                    fw.op("act", lambda e: e.activation(out=o.t[:], in_=pj.t[:], func=AF.Copy), reads=[pj.k], writes=[o.k])
                    fw.dma("sp", v_s[tt * 128:(tt + 1) * 128, vi * 512:(vi + 1) * 512], o.t[:], reads=[o.k], writes=[vk[vi]])
                proj_tm(Win, 4096 + vi * 512, 512, epi)
            if stop == "diff_proj":
                return
            fence()
            qm = big.t[:, 0:2 * T].rearrange("p (c t) -> p c t", c=2)
            km = big.t[:, 2 * T:4 * T].rearrange("p (c t) -> p c t", c=2)
            vh = big.t[:, 4 * T:6 * T].rearrange("p (tt d) -> p tt d", tt=16)
            tri1 = tri.t[:]
            for h in range(8):
                for c in range(2):
                    fw.dma("sp", qm[:, c, :], qT_s[2 * h + c], reads=[qk[2 * h + c]], writes=[big.k])
                    fw.dma("sp", km[:, c, :], kT_s[2 * h + c], reads=[kk[2 * h + c]], writes=[big.k])
                fw.dma("sp", vh, v_s[:, h * 256:(h + 1) * 256].rearrange("(tt p) d -> p tt d", p=128), reads=[vk[h // 2]], writes=[big.k])
                Sb = [P[0], P[1], P[5]]
                LA = 3
                pi = 0
                for qb in range(4):
                    a0 = [f512[0], f512[1]]
                    for c in range(2):
                        O0, O1, Db = P[2], P[3], P[4]
                        nk = 4 * qb + 4

                        def geom(kt):
                            q0 = max(kt * 128, qb * 512)
                            return q0, q0 - qb * 512

                        def S_emit(kt, base):
                            S = Sb[(base + kt) % 3]
                            q0, off = geom(kt)
                            mm(S.t[:, off:512], km[:, c, kt * 128:(kt + 1) * 128], qm[:, c, q0:(qb + 1) * 512], True, True, [big.k], [S.k])

                        def rest_emit(kt, base):
                            S = Sb[(base + kt) % 3]
                            pT = b512[(base + kt) % 4]
                            q0, off = geom(kt)
                            if off > 0:
                                fw.op("pool", lambda e: e.memset(pT.t[:, 0:off], 0.0), writes=[pT.k])
                            fw.op("act", lambda e: e.activation(out=pT.t[:, off:512], in_=S.t[:, off:512], func=AF.Exp, scale=SCALE), reads=[S.k], writes=[pT.k])
                            if kt * 128 >= qb * 512:
                                fw.op("dve", lambda e: e.tensor_tensor(out=pT.t[:, off:off + 128], in0=pT.t[:, off:off + 128], in1=tri1, op=ALU.mult),
                                      reads=[pT.k, tri.k], writes=[pT.k])
                            mm(O0.t[:], vh[:, kt, 0:128], pT.t[:], kt == 0, kt == nk - 1, [big.k, pT.k], [O0.k])
                            mm(O1.t[:], vh[:, kt, 128:256], pT.t[:], kt == 0, kt == nk - 1, [big.k, pT.k], [O1.k])
                            mm(Db.t[:], ones.t[:], pT.t[:], kt == 0, kt == nk - 1, [ones.k, pT.k], [Db.k])

                        s_em = 0
                        for kt in range(nk):
                            while s_em < min(nk, kt + LA):
                                S_emit(s_em, pi)
                                s_em += 1
                            rest_emit(kt, pi)
                        pi += nk
                        rden = f512[2]
                        osb = [f512[5], f512[6]]
                        fw.op("act", lambda e: e.activation(out=rden.t[:], in_=Db.t[:], func=AF.Ln), reads=[Db.k], writes=[rden.k])
                        fw.op("act", lambda e: e.activation(out=rden.t[:], in_=rden.t[:], func=AF.Exp, scale=-1.0), reads=[rden.k], writes=[rden.k])
                        fw.op("act", lambda e: e.activation(out=osb[0].t[:], in_=O0.t[:], func=AF.Copy), reads=[O0.k], writes=[osb[0].k])
                        fw.op("act", lambda e: e.activation(out=osb[1].t[:], in_=O1.t[:], func=AF.Copy), reads=[O1.k], writes=[osb[1].k])
                        if c == 0:
                            fw.op("dve", lambda e: e.tensor_tensor(out=a0[0].t[:], in0=osb[0].t[:], in1=rden.t[:], op=ALU.mult), reads=[osb[0].k, rden.k], writes=[a0[0].k])
                            fw.op("pool", lambda e: e.tensor_tensor(out=a0[1].t[:], in0=osb[1].t[:], in1=rden.t[:], op=ALU.mult), reads=[osb[1].k, rden.k], writes=[a0[1].k])
                        else:
                            fw.op("dve", lambda e: e.tensor_scalar(out=rden.t[:], in0=rden.t[:], scalar1=lamc.t[:, 4:5], scalar2=None, op0=ALU.mult),
                                  reads=[rden.k, lamc.k], writes=[rden.k])
                            for jj in range(2):
                                eng_ = "dve" if jj == 0 else "pool"
                                fw.op(eng_, lambda e: e.tensor_tensor(out=osb[jj].t[:], in0=osb[jj].t[:], in1=rden.t[:], op=ALU.mult), reads=[osb[jj].k, rden.k], writes=[osb[jj].k])
                                fw.op("pool", lambda e: e.tensor_tensor(out=a0[jj].t[:], in0=a0[jj].t[:], in1=osb[jj].t[:], op=ALU.add), reads=[a0[jj].k, osb[jj].k], writes=[a0[jj].k])
                    sqs = [b512[4], b512[5]]
                    for jj in range(2):
                        fw.op("act", lambda e: e.activation(out=sqs[jj].t[:], in_=a0[jj].t[:], func=AF.Square), reads=[a0[jj].k], writes=[sqs[jj].k])
                    Sq = P[5]
                    for jj in range(2):
                        mm(Sq.t[:], ones.t[:], sqs[jj].t[:], jj == 0, jj == 1, [ones.k, sqs[jj].k], [Sq.k])
                    rs = f512[4]
                    fw.op("act", lambda e: e.activation(out=rs.t[:], in_=Sq.t[:], func=AF.Ln, scale=1.0 / 256, bias=epsc.t[:, 0:1]), reads=[Sq.k], writes=[rs.k])
                    fw.op("act", lambda e: e.activation(out=rs.t[:], in_=rs.t[:], func=AF.Exp, scale=-0.5), reads=[rs.k], writes=[rs.k])
                    for jj in range(2):
                        ch = 2 * h + jj
                        fw.op("act", lambda e: e.activation(out=a0[jj].t[:], in_=a0[jj].t[:], func=AF.Identity, scale=gcol.t[:, 4 + jj:5 + jj]),
                              reads=[a0[jj].k, gcol.k], writes=[a0[jj].k])
                        fw.op("dve", lambda e: e.tensor_tensor(out=hT.t[:, ch, qb * 512:(qb + 1) * 512], in0=a0[jj].t[:], in1=rs.t[:], op=ALU.mult),
                              reads=[a0[jj].k, rs.k], writes=[hT_k[ch]])
            if stop == "diff_attn":
                return
            phase_outproj(l, diff_w_out[j])

        def dump_dbg():
            if not dbg:
                return
            fence()
            for tt in range(NT):
                b_ = xt[tt % 2]
                fw.dma("sp", b_.t[:], xres[tt * 128:(tt + 1) * 128, :], reads=xk[tt], writes=[b_.k])
                fw.dma("sp", dbg_out["dbg_x"][tt * 128:(tt + 1) * 128, :], b_.t[:], reads=[b_.k], is_output=True)
            for kc in range(KC):
                fw.dma("sp", dbg_out["dbg_h"][:, kc * T:(kc + 1) * T], hT.t[:, kc, :], reads=[hT_k[kc]], is_output=True)
            if stop not in ("ada", "norm0", "cols"):
                for i in range(16):
                    fw.dma("sp", dbg_out["dbg_q"][i], qT_s[i], reads=[qk[i]], is_output=True)
                    fw.dma("sp", dbg_out["dbg_k"][i], kT_s[i], reads=[kk[i]], is_output=True)
                for tt in range(NT):
                    _vc = 1024 if (stop or "").startswith("nsa") or stop in ("attn0", "mlp0") else 2048
                    fw.dma("sp", dbg_out["dbg_v"][tt * 128:(tt + 1) * 128, 0:_vc], v_s[tt * 128:(tt + 1) * 128, 0:_vc], reads=vk, is_output=True)

        for l in range(depth):
            if stop == "ada":
                break
            load_layer_cols(l)
            if stop == "cols":
                break
            phase_norm(l, 0)
            if stop == "norm%d" % l:
                break
            if l % 2 == 0:
                phase_nsa(l, l // 2)
            else:
                phase_diff(l, l // 2, 0.8 - 0.6 * math.exp(-0.3 * l))
            if stop in ("nsa_proj", "nsa_cmp", "nsa_attn", "diff_proj", "diff_attn") and ((l % 2 == 0) == stop.startswith("nsa")):
                break
            if stop == "attn%d" % l:
                break
            phase_norm(l, 1)
            phase_mlp(l, final=(l == depth - 1))
            if stop == "mlp%d" % l:
                break
        dump_dbg()
        fw.finish()
        print("instructions", fw.n_ins, "waits", fw.n_wait, "sems", fw.nsem, "sbuf_left", nc.sbuf_bytes_remaining)
    return nc


def kernel(**inputs):
    return run(inputs)


def run(inputs, depth=4, stop=None, dbg=False, ncores=None):
    consts = _consts()
    nc = build_program(depth=depth, stop=stop, dbg=dbg)
    B = inputs["x"].shape[0] if ncores is None else ncores
    shared = {k: np.ascontiguousarray(v) for k, v in inputs.items() if k not in ("x", "c", "positions")}
    in_maps = []
    for b in range(B):
        m = dict(shared)
        m["x"] = np.ascontiguousarray(inputs["x"][b])
        m["c"] = np.ascontiguousarray(inputs["c"][b])
        m["positions"] = np.ascontiguousarray(inputs["positions"][b]).astype(np.int32)
        m.update(consts)
        in_maps.append(m)
    res = run_bass_kernel_spmd(nc, in_maps, core_ids=list(range(B)))
    if dbg:
        return res.results
    return np.stack([np.asarray(r["out"], dtype=np.float32) for r in res.results], axis=0)
```

```python
import contextlib
import math
import numpy as np
import ml_dtypes
import concourse.bass as bass
import concourse.mybir as mybir
from concourse.bass_utils import run_bass_kernel_spmd

F32 = mybir.dt.float32
BF16 = mybir.dt.bfloat16
I32 = mybir.dt.int32
ALU = mybir.AluOpType
AF = mybir.ActivationFunctionType
AX = mybir.AxisListType

SEM_LIMIT = 30000
D = 2048
T = 2048
NT = 16
KC = 16
DFF = 8192
EPS = 1e-6
SCALE = 128 ** -0.5
NCMP = 127
BIGV = 1e30


class Tok:
    __slots__ = ("w", "r")

    def __init__(self):
        self.w = None
        self.r = {}


class FW:
    def __init__(self, nc, stack):
        self.nc = nc
        self.stack = stack
        self.eng = {"pe": nc.tensor, "dve": nc.vector, "act": nc.scalar,
                    "pool": nc.gpsimd, "sp": nc.sync}
        self.esem = {}
        self.ecnt = {}
        self.known = {k: {} for k in self.eng}
        self.nsem = 0
        for k in self.eng:
            self._new_esem(k)
        self.NDS = 6
        self.dsem = {q: [self._sem("d%s%d" % (q, i)) for i in range(self.NDS)] for q in ("sp", "pool")}
        self.dcnt = {q: [0] * self.NDS for q in self.dsem}
        self.drr = {q: 0 for q in self.dsem}
        self.out_events = []
        self.n_ins = 0
        self.n_wait = 0

    def _sem(self, name):
        self.nsem += 1
        return self.stack.enter_context(self.nc.semaphore("%s_%d" % (name, self.nsem)))

    def _new_esem(self, k):
        self.esem[k] = self._sem("e" + k)
        self.ecnt[k] = 0

    def _need(self, k, ev):
        if ev is None:
            return
        sem, val = ev
        if val <= 0:
            return
        kn = self.known[k]
        if kn.get(sem, 0) >= val:
            return
        if k == "pe" and sem is self.esem["pe"]:
            return
        self.eng[k].wait_ge(sem, val)
        self.n_wait += 1
        kn[sem] = val

    def _deps(self, k, reads, writes):
        for t in reads:
            self._need(k, t.w)
        for t in writes:
            self._need(k, t.w)
            for s, v in list(t.r.items()):
                self._need(k, (s, v))

    def _commit(self, ev, reads, writes):
        s, v = ev
        for t in reads:
            if t.r.get(s, 0) < v:
                t.r[s] = v
        for t in writes:
            t.w = ev
            t.r = {}

    def op(self, k, fn, reads=(), writes=()):
        self._deps(k, reads, writes)
        if self.ecnt[k] >= SEM_LIMIT:
            self._new_esem(k)
        ins = fn(self.eng[k])
        self.ecnt[k] += 1
        ins.then_inc(self.esem[k], 1)
        ev = (self.esem[k], self.ecnt[k])
        self._commit(ev, reads, writes)
        self.n_ins += 1
        return ev

    def dma(self, q, out, in_, reads=(), writes=(), is_output=False, **kw):
        self._deps(q, reads, writes)
        i = self.drr[q]
        self.drr[q] = (i + 1) % self.NDS
        if self.dcnt[q][i] + 16 > SEM_LIMIT:
            self._need(q, (self.dsem[q][i], self.dcnt[q][i]))
            self.dsem[q][i] = self._sem("d%s%d" % (q, i))
            self.dcnt[q][i] = 0
        sem = self.dsem[q][i]
        self._need(q, (sem, self.dcnt[q][i]))
        ins = self.eng[q].dma_start(out=out, in_=in_, **kw)
        self.dcnt[q][i] += 16
        ins.then_inc(sem, 16)
        ev = (sem, self.dcnt[q][i])
        self._commit(ev, reads, writes)
        if is_output:
            self.out_events.append(ev)
        self.n_ins += 1
        return ev

    def finish(self):
        for ev in self.out_events:
            self._need("sp", ev)
        for q in self.dsem:
            for i in range(self.NDS):
                if self.dcnt[q][i] > 0:
                    self._need("sp", (self.dsem[q][i], self.dcnt[q][i]))


class _View:
    def __init__(self, t, k):
        self.t = t
        self.k = k


class Buf:
    __slots__ = ("t", "k")

    def __init__(self, t):
        self.t = t
        self.k = Tok()


def _consts():
    bf = ml_dtypes.bfloat16
    c = {}
    c["c_ident"] = np.eye(128, dtype=np.float32).astype(bf)
    c["c_ones"] = np.ones((128, 128), np.float32).astype(bf)
    rm = np.zeros((128, 128), np.float32)
    for dp in range(64):
        rm[dp + 64, dp] = -1.0
        rm[dp, dp + 64] = 1.0
    c["c_rm"] = rm
    k = np.arange(128)[:, None]
    q = np.arange(128)[None, :]
    c["c_tri"] = (k <= q).astype(np.float32).astype(bf)
    c["c_low"] = (k > q).astype(np.float32).astype(bf)
    n = np.arange(128)[:, None]
    tq = np.arange(T)[None, :]
    mc = ((16 * n + 31) <= tq) & (n < NCMP)
    c["c_maskc"] = mc.astype(np.float32).astype(bf)
    E = np.zeros((32, 16, 128), np.float32)
    for kt in range(16):
        for kk in range(128):
            E[2 * kt + (1 if kk >= 64 else 0), kt, kk] = 1.0
    c["c_E"] = E.reshape(32, 16 * 128).astype(bf)
    cs = 16 * np.arange(NCMP)[:, None]
    ss = 64 * np.arange(32)[None, :]
    ov = np.minimum(cs + 32, ss + 64) - np.maximum(cs, ss)
    mp = np.zeros((128, 32), np.float32)
    mp[:NCMP] = np.clip(ov, 0, None) / 16
    c["c_map"] = mp.astype(bf)
    tt = np.arange(T)
    tblk = (tt // 64)[:, None]
    sid = np.arange(32)[None, :]
    valid = sid <= tblk
    forced = (sid == 0) | (sid == tblk) | (sid == tblk - 1)
    vu = (valid & ~forced).astype(np.float32)
    addc = np.where(forced, BIGV * (1.0 + sid / 64.0), np.where(valid, 0.0, -BIGV * (1.0 + sid / 64.0))).astype(np.float32)
    def tm(a):
        return np.ascontiguousarray(a.reshape(16, 128, 32).transpose(1, 0, 2).reshape(128, 16 * 32)).astype(np.float32)
    c["c_vu"] = tm(vu)
    c["c_addc"] = tm(addc)
    c["c_valid"] = tm(valid.astype(np.float32))
    half = 64
    inv = (10000.0 ** (-np.arange(half, dtype=np.float32) / half)).astype(np.float32)
    c["c_inv"] = np.concatenate([inv, inv]).reshape(128, 1).astype(np.float32)
    return c


CONST_SPECS = None


def build_program(depth=4, stop=None, dbg=False):
    nc = bass.Bass("TRN2", target_bir_lowering=False)

    def din(name, shape, dt=F32):
        return nc.dram_tensor(name, list(shape), dt, kind="ExternalInput").ap()

    def dscr(name, shape, dt):
        return nc.dram_tensor(name, list(shape), dt, kind="Internal").ap()

    x_in = din("x", [T, D])
    c_in = din("c", [D])
    pos_in = din("positions", [T], I32)
    ada_w = din("ada_w", [4, D, 6 * D])
    ada_b = din("ada_b", [4, 6 * D])
    attn_norm_g = din("attn_norm_g", [4, D])
    mlp_norm_g = din("mlp_norm_g", [4, D])
    mlp_w1 = din("mlp_w1", [4, D, DFF])
    mlp_w2 = din("mlp_w2", [4, DFF, D])
    nsa_w_in = din("nsa_w_in", [2, D, 5168])
    nsa_w_out = din("nsa_w_out", [2, D, D])
    nsa_q_norm = din("nsa_q_norm", [2, 128])
    nsa_k_norm = din("nsa_k_norm", [2, 3, 128])
    nsa_cmp_pos = din("nsa_cmp_pos", [2, 2, 32, 128])
    nsa_cmp_w1 = din("nsa_cmp_w1", [2, 2, 32, 128, 256])
    nsa_cmp_w2 = din("nsa_cmp_w2", [2, 2, 256, 128])
    diff_w_in = din("diff_w_in", [2, D, 6144])
    diff_w_out = din("diff_w_out", [2, D, D])
    diff_q_norm = din("diff_q_norm", [2, 128])
    diff_k_norm = din("diff_k_norm", [2, 128])
    diff_lambda = din("diff_lambda", [2, 4, 128])
    diff_sub_norm = din("diff_sub_norm", [2, 256])
    cst = {}
    for name, arr in _consts().items():
        cst[name] = din(name, arr.shape, BF16 if arr.dtype == ml_dtypes.bfloat16 else F32)

    out = nc.dram_tensor("out", [T, D], F32, kind="ExternalOutput").ap()
    xres = dscr("xres", [T, D], F32)
    modrow = dscr("modrow", [4, 6 * D], F32)
    qT_s = dscr("qT_s", [16, 128, T], BF16)
    kT_s = dscr("kT_s", [16, 128, T], BF16)
    v_s = dscr("v_s", [T, 2048], BF16)
    dbg_out = {}
    if dbg:
        for nm, shp, dt in (("dbg_h", [128, 16 * T], BF16), ("dbg_q", [16, 128, T], BF16), ("dbg_k", [16, 128, T], BF16),
                            ("dbg_v", [T, 2048], BF16), ("dbg_x", [T, D], F32), ("dbg_m", [4, 6 * D], F32)):
            dbg_out[nm] = nc.dram_tensor(nm, shp, dt, kind="ExternalOutput").ap()

    with contextlib.ExitStack() as st:
        fw = FW(nc, st)

        def sb(name, shape, dt):
            return Buf(st.enter_context(nc.sbuf_tensor(name, list(shape), dt)))

        def ps(name, shape, dt):
            return Buf(st.enter_context(nc.psum_tensor(name, list(shape), dt)))

        hT = sb("hT", [128, KC, T], BF16)
        hT_k = [Tok() for _ in range(KC)]
        NW = 8
        wb = [sb("wb%d" % i, [128, 2048], BF16) for i in range(NW)]
        wrr = [0]
        big = sb("big", [128, 32 * 512], BF16)
        xt = [_View(big.t[:, 0:4096].bitcast(F32), Tok()), _View(big.t[:, 4096:8192].bitcast(F32), Tok())]
        xn = [_View(big.t[:, 8192 + 2048 * j_:8192 + 2048 * (j_ + 1)], Tok()) for j_ in range(4)]
        fz = sb("fz", [128, 1], F32)

        def fence():
            fw.op("dve", lambda e: e.memset(fz.t[:], 0.0), writes=[fz.k, big.k, xt[0].k, xt[1].k] + [v_.k for v_ in xn])
        gbcs = [sb("gbc%d" % i, [128, 512], F32) for i in range(2)]
        gbi = [0]
        cosT = sb("cosT", [128, T], BF16)
        sinT = sb("sinT", [128, T], BF16)
        stg = xn[0:2]
        f512 = [sb("f512_%d" % i, [128, 512], F32) for i in range(8)]
        b512 = [sb("b512_%d" % i, [128, 512], BF16) for i in range(6)]
        xs = [sb("xs%d" % i, [128, 512], F32) for i in range(8)]
        small = sb("small", [128, 64], F32)
        colm = sb("colm", [128, 8, 16], F32)
        colk = Tok()
        ident = sb("ident", [128, 128], BF16)
        ones = sb("ones", [128, 128], BF16)
        rm32 = sb("rm32", [128, 128], F32)
        rg = [sb("rg%d" % i, [128, 128], BF16) for i in range(3)]
        gcol = sb("gcol", [128, 8], F32)
        tri = sb("tri", [128, 128], BF16)
        low = sb("low", [128, 128], BF16)
        maskc = sb("maskc", [128, T], BF16)
        Emat = sb("Emat", [32, 16 * 128], BF16)
        cmap = sb("cmap", [128, 32], BF16)
        vu = sb("vu", [128, 16 * 32], F32)
        addc = sb("addc", [128, 16 * 32], F32)
        validc = sb("validc", [128, 16 * 32], F32)
        invc = sb("invc", [128, 1], F32)
        gateT = sb("gateT", [48, T], BF16)
        kcT = sb("kcT", [128, 4, 128], BF16)
        vcm = sb("vcm", [128, 4, 128], BF16)
        selw = sb("selw", [128, 160], F32)
        selb = sb("selb", [128, 32], BF16)
        selT = sb("selT", [32, 128], BF16)
        selT2 = sb("selT2", [32, 128], BF16)
        msk = [sb("msk%d" % i, [128, 128], BF16) for i in range(4)]
        lamc = sb("lamc", [128, 8], F32)
        epsc = sb("epsc", [128, 1], F32)
        fw.op("dve", lambda e: e.memset(epsc.t[:], EPS), writes=[epsc.k])
        tinyc = sb("tinyc", [128, 1], F32)
        fw.op("dve", lambda e: e.memset(tinyc.t[:], 1e-30), writes=[tinyc.k])
        fw.op("act", lambda e: e.activation(out=lamc.t[:, 6:7], in_=tinyc.t[:, 0:1], func=AF.Copy), reads=[tinyc.k], writes=[lamc.k])

        P = [ps("P%d" % i, [128, 512], F32) for i in range(7)]
        PT = ps("PT", [128, 1024], BF16)
        _ptk = Tok()
        PTk = [_ptk, _ptk]

        xk = [[Tok() for _ in range(4)] for _ in range(NT)]
        modk = [Tok() for _ in range(4)]
        qk = [Tok() for _ in range(16)]
        kk = [Tok() for _ in range(16)]
        vk = [Tok() for _ in range(4)]

        fw.op("act", lambda e: e.activation(out=lamc.t[:, 7:8], in_=epsc.t[:, 0:1], func=AF.Copy), reads=[epsc.k], writes=[lamc.k])
        def mm(out_ap, lhsT, rhs, start, stop, reads, writes):
            return fw.op("pe", lambda e: e.matmul(out_ap, lhsT=lhsT, rhs=rhs, start=start, stop=stop),
                         reads=reads, writes=writes)

        def next_wb():
            i = wrr[0]
            wrr[0] = (i + 1) % NW
            return wb[i]

        class WStream:
            def __init__(self, loads, pf=3):
                self.loads = loads
                self.pf = pf
                self.bufs = []

            def get(self, i):
                while len(self.bufs) < min(len(self.loads), i + 1 + self.pf):
                    b = next_wb()
                    self.loads[len(self.bufs)](b)
                    self.bufs.append(b)
                return self.bufs[i]

        def load_const(buf, src):
            fw.dma("sp", buf.t[:], src, writes=[buf.k])

        load_const(ident, cst["c_ident"])
        load_const(ones, cst["c_ones"])
        load_const(rm32, cst["c_rm"])
        load_const(tri, cst["c_tri"])
        load_const(low, cst["c_low"])
        load_const(maskc, cst["c_maskc"])
        load_const(Emat, cst["c_E"])
        load_const(cmap, cst["c_map"])
        load_const(vu, cst["c_vu"])
        load_const(addc, cst["c_addc"])
        load_const(validc, cst["c_valid"])
        load_const(invc, cst["c_inv"])

        rmb = sb("rmb", [128, 128], BF16)
        fw.op("dve", lambda e: e.tensor_copy(out=rmb.t[:], in_=rm32.t[:]), reads=[rm32.k], writes=[rmb.k])
        def build_rope():
            ang = xt[0]
            tmp = xt[1]
            posi_t = xn[0].t[:].bitcast(I32)
            posk = xn[0].k
            for hf in range(2):
                cs = slice(hf * 1024, (hf + 1) * 1024)
                fw.dma("sp", posi_t, pos_in[hf * 1024:(hf + 1) * 1024].partition_broadcast(128), writes=[posk])
                fw.op("dve", lambda e: e.tensor_copy(out=ang.t[:, cs], in_=posi_t), reads=[posk], writes=[ang.k])
                fw.op("dve", lambda e: e.tensor_scalar(out=ang.t[:, cs], in0=ang.t[:, cs], scalar1=invc.t[:, 0:1], scalar2=None, op0=ALU.mult),
                      reads=[ang.k, invc.k], writes=[ang.k])
                for which, dst in ((0, sinT), (1, cosT)):
                    sh = 0.0 if which == 0 else math.pi / 2
                    fw.op("dve", lambda e: e.tensor_scalar(out=tmp.t[:, cs], in0=ang.t[:, cs], scalar1=sh, scalar2=1.0 / (2 * math.pi), op0=ALU.add, op1=ALU.mult),
                          reads=[ang.k], writes=[tmp.k])
                    fw.op("dve", lambda e: e.tensor_copy(out=posi_t, in_=tmp.t[:, cs]), reads=[tmp.k], writes=[posk])
                    fw.op("dve", lambda e: e.tensor_copy(out=tmp.t[:, cs], in_=posi_t), reads=[posk], writes=[tmp.k])
                    fw.op("dve", lambda e: e.scalar_tensor_tensor(out=tmp.t[:, cs], in0=tmp.t[:, cs], scalar=-2 * math.pi, in1=ang.t[:, cs], op0=ALU.mult, op1=ALU.add),
                          reads=[tmp.k, ang.k], writes=[tmp.k])
                    fw.op("dve", lambda e: e.tensor_scalar(out=tmp.t[:, cs], in0=tmp.t[:, cs], scalar1=sh, scalar2=None, op0=ALU.add), reads=[tmp.k], writes=[tmp.k])
                    fw.op("dve", lambda e: e.tensor_scalar(out=tmp.t[:, cs], in0=tmp.t[:, cs], scalar1=-math.pi, scalar2=math.pi, op0=ALU.max, op1=ALU.min),
                          reads=[tmp.k], writes=[tmp.k])
                    fw.op("act", lambda e: e.activation(out=dst.t[:, cs], in_=tmp.t[:, cs], func=AF.Sin), reads=[tmp.k], writes=[dst.k])

        build_rope()

        for tt in range(NT):
            b = xt[tt % 2]
            fw.dma("sp", b.t[:], x_in[tt * 128:(tt + 1) * 128, :], writes=[b.k])
            fw.dma("sp", xres[tt * 128:(tt + 1) * 128, :], b.t[:], reads=[b.k], writes=xk[tt])

        def phase_ada(nl):
            ccol = small
            cb = sb("cb", [128, 16], BF16)
            fw.dma("sp", ccol.t[:, 0:16], c_in.rearrange("(kc p) -> p kc", p=128), writes=[ccol.k], allow_slow_non_contiguous=True)
            fw.op("act", lambda e: e.activation(out=cb.t[:], in_=ccol.t[:, 0:16], func=AF.Silu), reads=[ccol.k], writes=[cb.k])
            class _V:
                pass
            brow = _V(); brow.t = xt[1].t[0:1, :]; brow.k = xt[1].k
            mrow = _V(); mrow.t = xt[0].t[0:1, :]; mrow.k = xt[0].k
            for l in range(nl):
                for eg in range(6):
                    c0 = eg * 2048
                    loads = []
                    for kc in range(KC):
                        def ld(b, kc=kc):
                            fw.dma("pool", b.t[:], ada_w[l, kc * 128:(kc + 1) * 128, c0:c0 + 2048], writes=[b.k])
                        loads.append(ld)
                    ws = WStream(loads)
                    for kc in range(KC):
                        b = ws.get(kc)
                        for j in range(4):
                            mm(P[j].t[0:1, :], cb.t[:, kc:kc + 1], b.t[:, j * 512:(j + 1) * 512], kc == 0, kc == KC - 1,
                               [cb.k, b.k], [P[j].k])
                    fw.dma("sp", brow.t, ada_b[l:l + 1, c0:c0 + 2048], writes=[brow.k])
                    for j in range(4):
                        fw.op("dve", lambda e: e.tensor_tensor(out=mrow.t[:, j * 512:(j + 1) * 512], in0=P[j].t[0:1, :],
                                                               in1=brow.t[:, j * 512:(j + 1) * 512], op=ALU.add),
                              reads=[P[j].k, brow.k], writes=[mrow.k])
                    fw.dma("sp", modrow[l:l + 1, c0:c0 + 2048], mrow.t, reads=[mrow.k], writes=[modk[l]])

        phase_ada(depth)
        if dbg:
            for l in range(depth):
                fw.dma("sp", small.t[0:1, 0:1], modrow[l:l + 1, 0:1], reads=[modk[l]], writes=[small.k])
            for l in range(depth):
                fw.dma("sp", dbg_out["dbg_m"][l:l + 1, :], modrow[l:l + 1, :], reads=[modk[l], small.k], is_output=True)

        def load_layer_cols(l):
            def col(dst_j, src_ap):
                fw.dma("sp", colm.t[:, dst_j, :], src_ap.rearrange("(kc p) -> p kc", p=128), reads=[modk[l]], writes=[colk],
                       allow_slow_non_contiguous=True)
            col(0, modrow[l, 0:D])
            col(1, modrow[l, D:2 * D])
            col(2, modrow[l, 3 * D:4 * D])
            col(3, modrow[l, 4 * D:5 * D])
            col(4, attn_norm_g[l, :])
            col(5, mlp_norm_g[l, :])
            for a, scj, gj in ((6, 1, 4), (7, 3, 5)):
                fw.op("dve", lambda e: e.scalar_tensor_tensor(out=colm.t[:, a, :], in0=colm.t[:, scj, :], scalar=1.0, in1=colm.t[:, gj, :],
                                                              op0=ALU.add, op1=ALU.mult), reads=[colk], writes=[colk])

        def load_gbc(l, which, dt):
            off = (2 * D if which == 0 else 5 * D) + dt * 512
            gbi[0] += 1
            g_ = gbcs[gbi[0] % 2]
            fw.dma("sp", g_.t[:], modrow[l, off:off + 512].partition_broadcast(128), reads=[modk[l]], writes=[g_.k])
            return g_

        def phase_norm(l, which):
            aj, shj = (6, 0) if which == 0 else (7, 2)
            fence()
            P6b = P[6].t[:].bitcast(BF16)
            banks = [(PT.t, PTk[0]), (P6b, P[6].k)]
            for tg in range(4):
                for j in range(4):
                    tt = tg * 4 + j
                    xb_ = xt[tt % 2]
                    xnb = xn[j]
                    fw.dma("sp", xb_.t[:], xres[tt * 128:(tt + 1) * 128, :], reads=xk[tt], writes=[xb_.k])
                    ssq = small.t[:, 32 + (tt % 2) * 4: 33 + (tt % 2) * 4]
                    rs = small.t[:, 34 + (tt % 2) * 4: 35 + (tt % 2) * 4]
                    fw.op("dve", lambda e: e.memset(ssq, 0.0), writes=[small.k])
                    fw.op("act", lambda e: e.activation(out=xnb.t[:], in_=xb_.t[:], func=AF.Square, accum_out=ssq),
                          reads=[xb_.k, small.k], writes=[xnb.k, small.k])
                    fw.op("act", lambda e: e.activation(out=rs, in_=ssq, func=AF.Sqrt, scale=1.0 / D, bias=epsc.t[:, 0:1]), reads=[small.k], writes=[small.k])
                    fw.op("dve", lambda e: e.reciprocal(out=rs, in_=rs), reads=[small.k], writes=[small.k])
                    fw.op("dve", lambda e: e.tensor_scalar(out=xnb.t[:], in0=xb_.t[:], scalar1=rs, scalar2=None, op0=ALU.mult),
                          reads=[xb_.k, small.k], writes=[xnb.k])
                for kc in range(KC):
                    bt, bk = banks[kc % 2]
                    for j in range(4):
                        fw.op("pe", lambda e: e.transpose(out=bt[:, j * 128:(j + 1) * 128], in_=xn[j].t[:, kc * 128:(kc + 1) * 128], identity=ident.t[:]),
                              reads=[xn[j].k, ident.k], writes=[bk])
                    fw.op("act", lambda e: e.activation(out=hT.t[:, kc, tg * 512:(tg + 1) * 512], in_=bt[:, 0:512], func=AF.Identity,
                                                        bias=colm.t[:, shj, kc:kc + 1], scale=colm.t[:, aj, kc:kc + 1]),
                          reads=[bk, colk], writes=[hT_k[kc]])

        def normrope(pj, n, gci, rgb, cos_ap, sin_ap, out_ap, out_reads, out_writes, wi):
            xb = b512[(2 * wi) % 6]
            sq = b512[(2 * wi + 1) % 6]
            t1 = f512[(3 * wi) % 6]
            t2 = f512[(3 * wi + 1) % 6]
            rsb = f512[(3 * wi + 2) % 6]
            prot = P[2 + (wi % 2)]
            pssq = P[4 + (wi % 2)]
            fw.op("act", lambda e: e.activation(out=xb.t[:, 0:n], in_=pj.t[:, 0:n], func=AF.Identity, scale=gcol.t[:, gci:gci + 1]),
                  reads=[pj.k, gcol.k], writes=[xb.k])
            fw.op("act", lambda e: e.activation(out=sq.t[:, 0:n], in_=pj.t[:, 0:n], func=AF.Square), reads=[pj.k], writes=[sq.k])

            def part2():
                mm(prot.t[:, 0:n], rmb.t[:], xb.t[:, 0:n], True, True, [rmb.k, xb.k], [prot.k])
                mm(pssq.t[:, 0:n], ones.t[:], sq.t[:, 0:n], True, True, [ones.k, sq.k], [pssq.k])
                fw.op("dve", lambda e: e.tensor_tensor(out=t1.t[:, 0:n], in0=xb.t[:, 0:n], in1=cos_ap, op=ALU.mult), reads=[xb.k, cosT.k], writes=[t1.k])
                fw.op("dve", lambda e: e.tensor_tensor(out=t2.t[:, 0:n], in0=prot.t[:, 0:n], in1=sin_ap, op=ALU.mult),
                      reads=[prot.k, sinT.k], writes=[t2.k])
                fw.op("act", lambda e: e.activation(out=rsb.t[:, 0:n], in_=pssq.t[:, 0:n], func=AF.Ln, scale=1.0 / 128, bias=epsc.t[:, 0:1]),
                      reads=[pssq.k], writes=[rsb.k])
                fw.op("act", lambda e: e.activation(out=rsb.t[:, 0:n], in_=rsb.t[:, 0:n], func=AF.Exp, scale=-0.5), reads=[rsb.k], writes=[rsb.k])
                fw.op("pool", lambda e: e.tensor_tensor(out=t1.t[:, 0:n], in0=t1.t[:, 0:n], in1=t2.t[:, 0:n], op=ALU.add), reads=[t1.k, t2.k], writes=[t1.k])
                fw.op("dve", lambda e: e.tensor_tensor(out=out_ap, in0=t1.t[:, 0:n], in1=rsb.t[:, 0:n], op=ALU.mult),
                      reads=[t1.k, rsb.k] + out_reads, writes=out_writes)
            return part2

        def make_rg(i, gci):
            return

        def load_gcol(j, src_ap):
            fw.dma("sp", gcol.t[:, j:j + 1], src_ap.rearrange("(p o) -> p o", o=1), writes=[gcol.k])

        def proj_fm(W3, col0, ws, wi, epilogue):
            b = ws.get(wi)
            bv = b.t[:].rearrange("p (kc n) -> p kc n", kc=KC)
            for tb in range(4):
                pj = P[tb % 2]
                for kc in range(KC):
                    mm(pj.t[:], bv[:, kc, :], hT.t[:, kc, tb * 512:(tb + 1) * 512], kc == 0, kc == KC - 1,
                       [b.k, hT_k[kc]], [pj.k])
                d_ = epilogue(tb, pj)
                flush_pending()
                if d_ is not None:
                    pending.append(d_)

        pending = []

        def flush_pending():
            for f_ in pending:
                f_()
            del pending[:]

        def fm_loads(W3, cols):
            loads = []
            for col0 in cols:
                def ld(b, col0=col0):
                    fw.dma("pool", b.t[:].rearrange("p (kc n) -> p kc n", kc=KC), W3[:, :, col0:col0 + 128], writes=[b.k])
                loads.append(ld)
            return loads

        def proj_tm(W3, col0, ncols, epilogue):
            loads = []
            for g4 in range(4):
                def ld(b, g4=g4):
                    fw.dma("pool", b.t[:, 0:4 * ncols].rearrange("p (kc n) -> p kc n", kc=4), W3[:, g4 * 4:(g4 + 1) * 4, col0:col0 + ncols], writes=[b.k])
                loads.append(ld)
            ws = WStream(loads, pf=3)
            bufs = [ws.get(i) for i in range(4)]
            for tt in range(NT):
                pj = P[tt % 2]
                for kc in range(KC):
                    b = bufs[kc // 4]
                    bv = b.t[:, 0:4 * ncols].rearrange("p (kc n) -> p kc n", kc=4)
                    mm(pj.t[:, 0:ncols], hT.t[:, kc, tt * 128:(tt + 1) * 128], bv[:, kc % 4, :], kc == 0, kc == KC - 1,
                       [b.k, hT_k[kc]], [pj.k])
                epilogue(tt, pj)

        def resid_load(tt, dt, slot):
            xsb = xs[slot % 8]
            fw.dma("sp", xsb.t[:], xres[tt * 128:(tt + 1) * 128, dt * 512:(dt + 1) * 512], reads=[xk[tt][dt]], writes=[xsb.k])

        def resid_mult(pj, g_, slot):
            tmp = f512[slot % 4]
            fw.op("dve", lambda e: e.tensor_tensor(out=tmp.t[:], in0=pj.t[:], in1=g_.t[:], op=ALU.mult),
                  reads=[pj.k, g_.k], writes=[tmp.k])

        def resid_store(tt, dt, slot, dst=None):
            xsb = xs[slot % 8]
            tmp = f512[slot % 4]
            fw.op("dve", lambda e: e.tensor_tensor(out=xsb.t[:], in0=xsb.t[:], in1=tmp.t[:], op=ALU.add), reads=[xsb.k, tmp.k], writes=[xsb.k])
            fw.dma("sp", xres[tt * 128:(tt + 1) * 128, dt * 512:(dt + 1) * 512], xsb.t[:], reads=[xsb.k], writes=[xk[tt][dt]])
            if dst is not None:
                fw.dma("sp", dst[tt * 128:(tt + 1) * 128, dt * 512:(dt + 1) * 512], xsb.t[:], reads=[xsb.k], is_output=True)

        def phase_outproj(l, Wout):
            W3 = Wout.rearrange("(kc p) n -> p kc n", p=128)
            slot = [0]
            for dt in range(4):
                g_ = load_gbc(l, 0, dt)
                resid_load(0, dt, slot[0])
                resid_load(1, dt, slot[0] + 1)

                def epi(tt, pj, dt=dt, g_=g_):
                    sl = slot[0]
                    if tt + 2 < NT:
                        resid_load(tt + 2, dt, sl + 2)
                    resid_mult(pj, g_, sl)
                    resid_store(tt, dt, sl)
                    slot[0] += 1
                proj_tm(W3, dt * 512, 512, epi)

        def phase_mlp(l, final):
            W1 = mlp_w1[l].rearrange("(kc p) n -> p kc n", p=128)
            W2 = mlp_w2[l].rearrange("(fc p) n -> p fc n", p=128)
            hid = big.t[:].rearrange("p (fc n) -> p fc n", fc=16)
            fence()
            loads = []
            for tb2 in range(2):
                for part in range(4):
                    for g4 in range(4):
                        c0 = (part * 16 + g4 * 4) * 128
                        for q in range(4):
                            def ld(b, q=q, c0=c0):
                                fw.dma("pool", b.t[:].rearrange("p (kc n) -> p kc n", kc=4), W1[:, 4 * q:4 * q + 4, c0:c0 + 512], writes=[b.k])
                            loads.append(ld)
                    for dt in range(4):
                        for q in range(4):
                            def ld(b, q=q, dt=dt, part=part):
                                fw.dma("pool", b.t[:].rearrange("p (fc n) -> p fc n", fc=4),
                                       W2[:, part * 16 + 4 * q: part * 16 + 4 * q + 4, dt * 512:(dt + 1) * 512], writes=[b.k])
                            loads.append(ld)
            ws = WStream(loads, pf=4)
            wi = 0
            cnt = 0
            slot = 0
            for tb2 in range(2):
                tok0 = tb2 * 1024
                for part in range(4):
                    for g4 in range(4):
                        tl = [ws.get(wi + q) for q in range(4)]
                        wi += 4
                        tv = [b.t[:].rearrange("p (kc n) -> p kc n", kc=4) for b in tl]
                        for fcl in range(4):
                            fc = g4 * 4 + fcl
                            for tbb in range(2):
                                pj = P[cnt % 2]
                                r = b512[cnt % 6]
                                cnt += 1
                                for kc in range(KC):
                                    mm(pj.t[:], tv[kc // 4][:, kc % 4, fcl * 128:(fcl + 1) * 128],
                                       hT.t[:, kc, tok0 + tbb * 512: tok0 + (tbb + 1) * 512], kc == 0, kc == KC - 1,
                                       [tl[kc // 4].k, hT_k[kc]], [pj.k])
                                fw.op("act", lambda e: e.activation(out=r.t[:], in_=pj.t[:], func=AF.Relu), reads=[pj.k], writes=[r.k])
                                fw.op("dve", lambda e: e.tensor_tensor(out=hid[:, fc, tbb * 512:(tbb + 1) * 512], in0=r.t[:], in1=r.t[:], op=ALU.mult),
                                      reads=[r.k], writes=[big.k])
                    for dt in range(4):
                        g_ = load_gbc(l, 1, dt)
                        tl = [ws.get(wi + q) for q in range(4)]
                        wi += 4
                        tv = [b.t[:].rearrange("p (fc n) -> p fc n", fc=4) for b in tl]
                        for ps_ in range(2):
                            for t4 in range(4):
                                resid_load(tb2 * 8 + ps_ * 4 + t4, dt, slot + ps_ * 4 + t4)
                        for ps_ in range(2):
                            for fc in range(16):
                                for t4 in range(4):
                                    tl_ = ps_ * 4 + t4
                                    mm(P[2 + t4].t[:], hid[:, fc, tl_ * 128:(tl_ + 1) * 128], tv[fc // 4][:, fc % 4, :], fc == 0, fc == 15,
                                       [big.k, tl[fc // 4].k], [P[2 + t4].k])
                            for t4 in range(4):
                                resid_mult(P[2 + t4], g_, slot + ps_ * 4 + t4)
                            for t4 in range(4):
                                resid_store(tb2 * 8 + ps_ * 4 + t4, dt, slot + ps_ * 4 + t4, dst=(out if (final and part == 3) else None))
                        slot += 8

        def phase_nsa(l, j):
            Win = nsa_w_in[j].rearrange("(kc p) n -> p kc n", p=128)
            load_gcol(0, nsa_q_norm[j, :])
            for i in range(3):
                load_gcol(1 + i, nsa_k_norm[j, i, :])
            make_rg(0, 0)
            make_rg(1, 2)
            make_rg(2, 3)
            chunks = []
            for h in range(16):
                chunks.append((h * 128, "nr", qT_s, qk, h, 0, 0))
            for g in range(4):
                chunks.append((2048 + g * 128, "raw", kT_s, kk, g, None, None))
            for g in range(4):
                chunks.append((2560 + g * 128, "raw", kT_s, kk, 4 + g, None, None))
            for g in range(4):
                chunks.append((3072 + g * 128, "nr", kT_s, kk, 8 + g, 2, 1))
            for g in range(4):
                chunks.append((4096 + g * 128, "nr", kT_s, kk, 12 + g, 3, 2))
            ws = WStream(fm_loads(Win, [c[0] for c in chunks]), pf=3)
            cnt = [0]
            for ci, (col0, kind, dst, dk, idx, gci, rgi) in enumerate(chunks):
                sg = stg[ci % 2]

                def epi(tb, pj, kind=kind, gci=gci, rgi=rgi, sg=sg):
                    o_ap = sg.t[:, tb * 512:(tb + 1) * 512]
                    if kind == "raw":
                        fw.op("act", lambda e: e.activation(out=o_ap, in_=pj.t[:], func=AF.Copy), reads=[pj.k], writes=[sg.k])
                        return None
                    else:
                        p2 = normrope(pj, 512, gci, rg[rgi], cosT.t[:, tb * 512:(tb + 1) * 512], sinT.t[:, tb * 512:(tb + 1) * 512],
                                      o_ap, [], [sg.k], cnt[0])
                        cnt[0] += 1
                        return p2
                proj_fm(Win, col0, ws, ci, epi)
                pending.append(lambda dst=dst, idx=idx, sg=sg, dk=dk: fw.dma("sp", dst[idx], sg.t[:], reads=[sg.k], writes=[dk[idx]]))
            flush_pending()
            for vi, col0 in enumerate((3584, 4608)):
                def epi(tt, pj, vi=vi):
                    o = b512[tt % 6]
                    fw.op("act", lambda e: e.activation(out=o.t[:], in_=pj.t[:], func=AF.Copy), reads=[pj.k], writes=[o.k])
                    fw.dma("sp", v_s[tt * 128:(tt + 1) * 128, vi * 512:(vi + 1) * 512], o.t[:], reads=[o.k], writes=[vk[vi]])
                proj_tm(Win, col0, 512, epi)
            gb = next_wb()
            fw.dma("pool", gb.t[:, 0:KC * 48].rearrange("p (kc n) -> p kc n", kc=KC), Win[:, :, 5120:5168], writes=[gb.k])
            gv = gb.t[:, 0:KC * 48].rearrange("p (kc n) -> p kc n", kc=KC)
            for tb in range(4):
                pj = P[tb % 2]
                for kc in range(KC):
                    mm(pj.t[0:48, :], gv[:, kc, :], hT.t[:, kc, tb * 512:(tb + 1) * 512], kc == 0, kc == KC - 1, [gb.k, hT_k[kc]], [pj.k])
                fw.op("act", lambda e: e.activation(out=gateT.t[:, tb * 512:(tb + 1) * 512], in_=pj.t[0:48, :], func=AF.Sigmoid),
                      reads=[pj.k], writes=[gateT.k])
            if stop == "nsa_proj":
                return
            fence()
            kall = big.t[:, 0:4 * T].rearrange("p (g t) -> p g t", g=4)
            hidc = big.t[:, 4 * T:4 * T + 2 * 512].rearrange("p (fc n) -> p fc n", fc=2)
            posT = small.t[:, 0:32]
            posTb = b512[0]
            fw.op("dve", lambda e: e.memset(kcT.t[:], 0.0), writes=[kcT.k])
            fw.op("dve", lambda e: e.memset(vcm.t[:], 0.0), writes=[vcm.k])
            for kv in range(2):
                for g in range(4):
                    fw.dma("sp", kall[:, g, :], kT_s[kv * 4 + g], reads=[kk[kv * 4 + g]], writes=[big.k])
                fw.dma("sp", posT, nsa_cmp_pos[j, kv].rearrange("l d -> d l"), writes=[small.k], allow_slow_non_contiguous=True)
                fw.op("dve", lambda e: e.tensor_copy(out=posTb.t[:, 0:32], in_=posT), reads=[small.k], writes=[posTb.k])
                w1v = nsa_cmp_w1[j, kv].rearrange("l d f -> d l f")
                w1b = []
                for q4 in range(4):
                    b = next_wb()
                    fw.dma("pool", b.t[:].rearrange("p (l f) -> p l f", l=8), w1v[:, q4 * 8:(q4 + 1) * 8, :], writes=[b.k])
                    w1b.append(b)
                w2b = next_wb()
                fw.dma("pool", w2b.t[:, 0:256].rearrange("p (fc d) -> p fc d", fc=2), nsa_cmp_w2[j, kv].rearrange("(fc p) d -> p fc d", p=128), writes=[w2b.k])
                w2v = w2b.t[:, 0:256].rearrange("p (fc d) -> p fc d", fc=2)
                biasc = small.t[:, 40:42]
                for fc in range(2):
                    pb_ = P[4]
                    for lq in range(32):
                        bv = w1b[lq // 8].t[:].rearrange("p (l f) -> p l f", l=8)
                        mm(pb_.t[:, 0:1], bv[:, lq % 8, fc * 128:(fc + 1) * 128], posTb.t[:, lq:lq + 1], lq == 0, lq == 31,
                           [w1b[lq // 8].k, posTb.k], [pb_.k])
                    fw.op("dve", lambda e: e.tensor_copy(out=small.t[:, 40 + fc:41 + fc], in_=pb_.t[:, 0:1]), reads=[pb_.k], writes=[small.k])
                    ph = P[fc]
                    for lq in range(32):
                        bv = w1b[lq // 8].t[:].rearrange("p (l f) -> p l f", l=8)
                        mm(ph.t[:, 0:508].rearrange("p (g n) -> p g n", g=4), bv[:, lq % 8, fc * 128:(fc + 1) * 128],
                           kall[:, :, lq:lq + 16 * (NCMP - 1) + 1:16], lq == 0, lq == 31, [w1b[lq // 8].k, big.k], [ph.k])
                    fw.op("act", lambda e: e.activation(out=hidc[:, fc, 0:508], in_=ph.t[:, 0:508], func=AF.Silu, bias=small.t[:, 40 + fc:41 + fc]),
                          reads=[ph.k, small.k], writes=[big.k])
                if kv == 0:
                    make_rg(1, 1)
                    pk = P[6]
                    for fc in range(2):
                        mm(pk.t[:, 0:508], w2v[:, fc, :], hidc[:, fc, 0:508], fc == 0, fc == 1, [w2b.k, big.k], [pk.k])
                    for g in range(4):
                        v = _View(pk.t[:, g * NCMP:(g + 1) * NCMP], pk.k)
                        normrope(v, NCMP, 1, rg[1], cosT.t[:, 31:31 + 16 * (NCMP - 1) + 1:16], sinT.t[:, 31:31 + 16 * (NCMP - 1) + 1:16],
                                 kcT.t[:, g, 0:NCMP], [], [kcT.k], g)()
                    make_rg(1, 2)
                else:
                    for g in range(4):
                        pv = P[6]
                        for fc in range(2):
                            mm(pv.t[0:NCMP, 0:128], hidc[:, fc, g * NCMP:(g + 1) * NCMP], w2v[:, fc, :], fc == 0, fc == 1, [big.k, w2b.k], [pv.k])
                        fw.op("act", lambda e: e.activation(out=vcm.t[0:NCMP, g, :], in_=pv.t[0:NCMP, 0:128], func=AF.Copy), reads=[pv.k], writes=[vcm.k])
            if stop == "nsa_cmp":
                return
            qall = big.t[:, 0:4 * T].rearrange("p (qt r q) -> p qt r q", qt=16, r=4)
            ksT = big.t[:, 4 * T:5 * T]
            kwT = big.t[:, 5 * T:6 * T]
            vsb = big.t[:, 6 * T:7 * T].rearrange("p (tt d) -> p tt d", tt=16)
            vwb = big.t[:, 7 * T:8 * T].rearrange("p (tt d) -> p tt d", tt=16)
            vu3 = vu.t[:].rearrange("p (qt s) -> p qt s", qt=16)
            addc3 = addc.t[:].rearrange("p (qt s) -> p qt s", qt=16)
            val3 = validc.t[:].rearrange("p (qt s) -> p qt s", qt=16)
            E3 = Emat.t[:].rearrange("s (kt k) -> s kt k", kt=16)
            PTf = PT.t[:].bitcast(F32)
            Sb = [P[0], P[1], P[5]]
            LA = 3
            LE = 2
            selTs = [selT, selT2]
            r4 = lambda ap: ap.rearrange("p (r q) -> p r q", r=4)
            for g in range(4):
                for r in range(4):
                    fw.dma("sp", qall[:, :, r, :], qT_s[g * 4 + r].rearrange("p (qt q) -> p qt q", q=128), reads=[qk[g * 4 + r]], writes=[big.k])
                fw.dma("sp", ksT, kT_s[8 + g], reads=[kk[8 + g]], writes=[big.k])
                fw.dma("sp", kwT, kT_s[12 + g], reads=[kk[12 + g]], writes=[big.k])
                fw.dma("sp", vsb, v_s[:, g * 128:(g + 1) * 128].rearrange("(tt p) d -> p tt d", p=128), reads=[vk[0]], writes=[big.k])
                fw.dma("sp", vwb, v_s[:, 512 + g * 128:512 + (g + 1) * 128].rearrange("(tt p) d -> p tt d", p=128), reads=[vk[1]], writes=[big.k])
                cnt_i = [0]

                def finish_branch(x, qt, accb):
                    Ob, Db = P[2], P[3]
                    osb = f512[2]
                    rden = f512[3]
                    cf = f512[4]
                    fw.op("act", lambda e: e.activation(out=rden.t[:], in_=Db.t[:], func=AF.Ln, bias=tinyc.t[:, 0:1]), reads=[Db.k], writes=[rden.k])
                    fw.op("act", lambda e: e.activation(out=osb.t[:], in_=Ob.t[:], func=AF.Copy), reads=[Ob.k], writes=[osb.k])
                    fw.op("act", lambda e: e.activation(out=rden.t[:], in_=rden.t[:], func=AF.Exp, scale=-1.0), reads=[rden.k], writes=[rden.k])
                    Gb = P[6]
                    for r in range(4):
                        cidx = x * 16 + g * 4 + r
                        mm(Gb.t[:, r * 128:(r + 1) * 128], ident.t[0:48, cidx:cidx + 1].to_broadcast([48, 128]),
                           gateT.t[:, qt * 128:(qt + 1) * 128], True, True, [ident.k, gateT.k], [Gb.k])
                    fw.op("dve", lambda e: e.tensor_tensor(out=cf.t[:], in0=Gb.t[:], in1=rden.t[:], op=ALU.mult), reads=[Gb.k, rden.k], writes=[cf.k])
                    if x == 0:
                        fw.op("pool", lambda e: e.tensor_tensor(out=accb.t[:], in0=osb.t[:], in1=cf.t[:], op=ALU.mult), reads=[osb.k, cf.k], writes=[accb.k])
                    else:
                        fw.op("dve", lambda e: e.tensor_tensor(out=cf.t[:], in0=osb.t[:], in1=cf.t[:], op=ALU.mult), reads=[osb.k, cf.k], writes=[cf.k])
                        fw.op("pool", lambda e: e.tensor_tensor(out=accb.t[:], in0=accb.t[:], in1=cf.t[:], op=ALU.add), reads=[accb.k, cf.k], writes=[accb.k])
                    return rden

                class StageA:
                    def __init__(self, qt):
                        self.qt = qt
                        self.accb = f512[5 + (qt % 2)]
                        self.selTb = selTs[qt % 2]
                        self.pT = b512[5]
                        self.pn = b512[4]
                        self.step = 0

                    def advance(self, upto):
                        while self.step < upto and self.step < 4:
                            [self.s_e, self.pv_finish, self.imp_topk, self.transpose][self.step]()
                            self.step += 1

                    def s_e(self):
                        qt = self.qt
                        S = Sb[cnt_i[0] % 3]
                        cnt_i[0] += 1
                        mm(r4(S.t[:]), kcT.t[:, g, :], qall[:, qt, :, :], True, True, [kcT.k, big.k], [S.k])
                        pT = self.pT
                        fw.op("act", lambda e: e.activation(out=pT.t[:], in_=S.t[:], func=AF.Exp, scale=SCALE), reads=[S.k], writes=[pT.k])
                        mc = maskc.t[:, qt * 128:(qt + 1) * 128].unsqueeze(1).to_broadcast([128, 4, 128])
                        fw.op("dve", lambda e: e.tensor_tensor(out=r4(pT.t[:]), in0=r4(pT.t[:]), in1=mc, op=ALU.mult), reads=[pT.k, maskc.k], writes=[pT.k])

                    def pv_finish(self):
                        pT = self.pT
                        mm(P[2].t[:], vcm.t[:, g, :], pT.t[:], True, True, [vcm.k, pT.k], [P[2].k])
                        mm(P[3].t[:], ones.t[:], pT.t[:], True, True, [ones.k, pT.k], [P[3].k])
                        rden = finish_branch(0, self.qt, self.accb)
                        pn = self.pn
                        fw.op("dve", lambda e: e.tensor_tensor(out=pn.t[:], in0=pT.t[:], in1=rden.t[:], op=ALU.mult), reads=[pT.k, rden.k], writes=[pn.k])

                    def imp_topk(self):
                        qt = self.qt
                        pn = self.pn
                        Pi = PTf[:, 256:288]
                        for r in range(4):
                            mm(Pi, pn.t[:, r * 128:(r + 1) * 128], cmap.t[:], r == 0, r == 3, [pn.k, cmap.k], [PTk[0]])
                        iv = selw.t[:, 0:32]
                        iv2 = selw.t[:, 32:64]
                        m8a = selw.t[:, 64:72]
                        m8b = selw.t[:, 72:80]
                        fw.op("dve", lambda e: e.tensor_tensor(out=iv, in0=Pi, in1=vu3[:, qt, :], op=ALU.mult), reads=[PTk[0], vu.k], writes=[selw.k])
                        fw.op("dve", lambda e: e.tensor_tensor(out=iv, in0=iv, in1=addc3[:, qt, :], op=ALU.add), reads=[selw.k, addc.k], writes=[selw.k])
                        fw.op("dve", lambda e: e.max(out=m8a, in_=iv), reads=[selw.k], writes=[selw.k])
                        fw.op("dve", lambda e: e.match_replace(out=iv2, in_to_replace=m8a, in_values=iv, imm_value=-3.0e38), reads=[selw.k], writes=[selw.k])
                        fw.op("dve", lambda e: e.max(out=m8b, in_=iv2), reads=[selw.k], writes=[selw.k])
                        fw.op("dve", lambda e: e.tensor_scalar(out=iv2, in0=iv, scalar1=selw.t[:, 79:80], scalar2=None, op0=ALU.is_ge), reads=[selw.k], writes=[selw.k])
                        fw.op("dve", lambda e: e.tensor_tensor(out=selb.t[:], in0=iv2, in1=val3[:, qt, :], op=ALU.mult), reads=[selw.k, validc.k], writes=[selb.k])

                    def transpose(self):
                        selTb = self.selTb
                        fw.op("pe", lambda e: e.transpose(out=PT.t[0:32, 0:128], in_=selb.t[:], identity=ident.t[:]), reads=[selb.k, ident.k], writes=[PTk[0]])
                        fw.op("act", lambda e: e.activation(out=selTb.t[:], in_=PT.t[0:32, 0:128], func=AF.Copy), reads=[PTk[0]], writes=[selTb.k])

                def stage_B(qt, A):
                    accb = f512[5 + (qt % 2)]
                    selTb = selTs[qt % 2]
                    qcols = qall[:, qt, :, :]
                    items = []
                    wl = list(range(max(0, qt - 4), qt + 1))
                    for ii, kt in enumerate(wl):
                        items.append((2, kt, ii == 0, ii == len(wl) - 1))
                    nwin = len(items)
                    for kt in range(qt + 1):
                        items.append((1, kt, kt == 0, kt == qt))
                    n_it = len(items)
                    base = cnt_i[0]
                    cnt_i[0] += n_it

                    def S_emit(i):
                        x, kt, fst, lst = items[i]
                        S = Sb[(base + i) % 3]
                        kap = ksT[:, kt * 128:(kt + 1) * 128] if x == 1 else kwT[:, kt * 128:(kt + 1) * 128]
                        mm(r4(S.t[:]), kap, qcols, True, True, [big.k], [S.k])
                        if x == 1:
                            sl = ((base + i) % 4) * 128
                            mm(P[4].t[:, sl:sl + 128], E3[:, kt, :], selTb.t[:], True, True, [Emat.k, selTb.k], [P[4].k])
                            if kt == qt:
                                mb = msk[(base + i) % 4]
                                fw.op("dve", lambda e: e.tensor_tensor(out=mb.t[:], in0=P[4].t[:, sl:sl + 128], in1=tri.t[:], op=ALU.mult),
                                      reads=[P[4].k, tri.k], writes=[mb.k])

                    def exp_emit(i):
                        x, kt, fst, lst = items[i]
                        S = Sb[(base + i) % 3]
                        pT = b512[(base + i) % 4]
                        fw.op("act", lambda e: e.activation(out=pT.t[:], in_=S.t[:], func=AF.Exp, scale=SCALE), reads=[S.k], writes=[pT.k])
                        mk = None
                        if x == 1:
                            if kt == qt:
                                mk = (msk[(base + i) % 4].t[:].unsqueeze(1).to_broadcast([128, 4, 128]), [msk[(base + i) % 4].k])
                            else:
                                sl = ((base + i) % 4) * 128
                                mk = (P[4].t[:, sl:sl + 128].unsqueeze(1).to_broadcast([128, 4, 128]), [P[4].k])
                        elif kt == qt:
                            mk = (tri.t[:].unsqueeze(1).to_broadcast([128, 4, 128]), [tri.k])
                        elif kt == qt - 4:
                            mk = (low.t[:].unsqueeze(1).to_broadcast([128, 4, 128]), [low.k])
                        if mk is not None:
                            fw.op("dve", lambda e: e.tensor_tensor(out=r4(pT.t[:]), in0=r4(pT.t[:]), in1=mk[0], op=ALU.mult),
                                  reads=[pT.k] + mk[1], writes=[pT.k])

                    def pv_emit(i):
                        x, kt, fst, lst = items[i]
                        pT = b512[(base + i) % 4]
                        vap = vsb[:, kt, :] if x == 1 else vwb[:, kt, :]
                        mm(P[2].t[:], vap, pT.t[:], fst, lst, [big.k, pT.k], [P[2].k])
                        mm(P[3].t[:], ones.t[:], pT.t[:], fst, lst, [ones.k, pT.k], [P[3].k])
                        if lst:
                            finish_branch(x, qt, accb)

                    if A is not None:
                        A.advance(1)
                    s_em = 0
                    e_em = 0
                    for i in range(n_it):
                        while s_em < min(n_it, i + LA):
                            S_emit(s_em)
                            s_em += 1
                        while e_em < min(n_it, i + LE):
                            exp_emit(e_em)
                            e_em += 1
                        pv_emit(i)
                        if A is not None:
                            if i == nwin - 1:
                                A.advance(2)
                            elif i == nwin + 1:
                                A.advance(3)
                            elif i == nwin + 3:
                                A.advance(4)
                    if A is not None:
                        A.advance(4)
                    for r in range(4):
                        h = g * 4 + r
                        fw.op("act", lambda e: e.activation(out=hT.t[:, h, qt * 128:(qt + 1) * 128], in_=accb.t[:, r * 128:(r + 1) * 128], func=AF.Copy),
                              reads=[accb.k], writes=[hT_k[h]])

                A0 = StageA(0)
                A0.advance(4)
                for qt in range(NT):
                    A = StageA(qt + 1) if qt + 1 < NT else None
                    stage_B(qt, A)
            if stop == "nsa_attn":
                return
            phase_outproj(l, nsa_w_out[j])

        def phase_diff(l, j, lam_init):
            Win = diff_w_in[j].rearrange("(kc p) n -> p kc n", p=128)
            load_gcol(0, diff_q_norm[j, :])
            load_gcol(1, diff_k_norm[j, :])
            load_gcol(4, diff_sub_norm[j, 0:128])
            load_gcol(5, diff_sub_norm[j, 128:256])
            make_rg(0, 0)
            make_rg(1, 1)
            lb = f512[0]
            fw.dma("sp", lb.t[:], diff_lambda[j].rearrange("a d -> (a d)").partition_broadcast(128), writes=[lb.k])
            fw.op("dve", lambda e: e.tensor_tensor(out=lb.t[:, 0:128], in0=lb.t[:, 0:128], in1=lb.t[:, 128:256], op=ALU.mult), reads=[lb.k], writes=[lb.k])
            fw.op("dve", lambda e: e.tensor_tensor(out=lb.t[:, 256:384], in0=lb.t[:, 256:384], in1=lb.t[:, 384:512], op=ALU.mult), reads=[lb.k], writes=[lb.k])
            fw.op("dve", lambda e: e.reduce_sum(out=lamc.t[:, 0:1], in_=lb.t[:, 0:128], axis=AX.X), reads=[lb.k], writes=[lamc.k])
            fw.op("dve", lambda e: e.reduce_sum(out=lamc.t[:, 1:2], in_=lb.t[:, 256:384], axis=AX.X), reads=[lb.k], writes=[lamc.k])
            fw.op("act", lambda e: e.activation(out=lamc.t[:, 2:4], in_=lamc.t[:, 0:2], func=AF.Exp), reads=[lamc.k], writes=[lamc.k])
            fw.op("dve", lambda e: e.tensor_tensor(out=lamc.t[:, 4:5], in0=lamc.t[:, 3:4], in1=lamc.t[:, 2:3], op=ALU.subtract), reads=[lamc.k], writes=[lamc.k])
            fw.op("dve", lambda e: e.tensor_scalar(out=lamc.t[:, 4:5], in0=lamc.t[:, 4:5], scalar1=-lam_init, scalar2=None, op0=ALU.add), reads=[lamc.k], writes=[lamc.k])
            fw.op("dve", lambda e: e.tensor_scalar(out=gcol.t[:, 4:6], in0=gcol.t[:, 4:6], scalar1=(1.0 - lam_init), scalar2=None, op0=ALU.mult),
                  reads=[gcol.k], writes=[gcol.k])
            chunks = []
            for m in range(16):
                chunks.append((m * 128, qT_s, qk, m, 0, 0))
            for m in range(16):
                chunks.append((2048 + m * 128, kT_s, kk, m, 1, 1))
            ws = WStream(fm_loads(Win, [c[0] for c in chunks]), pf=3)
            cnt = [0]
            for ci, (col0, dst, dk, idx, gci, rgi) in enumerate(chunks):
                sg = stg[ci % 2]

                def epi(tb, pj, gci=gci, rgi=rgi, sg=sg):
                    p2 = normrope(pj, 512, gci, rg[rgi], cosT.t[:, tb * 512:(tb + 1) * 512], sinT.t[:, tb * 512:(tb + 1) * 512],
                                  sg.t[:, tb * 512:(tb + 1) * 512], [], [sg.k], cnt[0])
                    cnt[0] += 1
                    return p2
                proj_fm(Win, col0, ws, ci, epi)
                pending.append(lambda dst=dst, idx=idx, sg=sg, dk=dk: fw.dma("sp", dst[idx], sg.t[:], reads=[sg.k], writes=[dk[idx]]))
            flush_pending()
            for vi in range(4):
                def epi(tt, pj, vi=vi):
                    o = b512[tt % 6]
                    fw.op("act", lambda e: e.activation(out=o.t[:], in_=pj.t[:], func=AF.Copy), reads=[pj.k], writes=[o.k])
                    fw.dma("sp", v_s[tt * 128:(tt + 1) * 128, vi * 512:(vi + 1) * 512], o.t[:], reads=[o.k], writes=[vk[vi]])
                proj_tm(Win, 4096 + vi * 512, 512, epi)
            if stop == "diff_proj":
                return
            fence()
            qm = big.t[:, 0:2 * T].rearrange("p (c t) -> p c t", c=2)
            km = big.t[:, 2 * T:4 * T].rearrange("p (c t) -> p c t", c=2)
            vh = big.t[:, 4 * T:6 * T].rearrange("p (tt d) -> p tt d", tt=16)
            tri1 = tri.t[:]
            for h in range(8):
                for c in range(2):
                    fw.dma("sp", qm[:, c, :], qT_s[2 * h + c], reads=[qk[2 * h + c]], writes=[big.k])
                    fw.dma("sp", km[:, c, :], kT_s[2 * h + c], reads=[kk[2 * h + c]], writes=[big.k])
                fw.dma("sp", vh, v_s[:, h * 256:(h + 1) * 256].rearrange("(tt p) d -> p tt d", p=128), reads=[vk[h // 2]], writes=[big.k])
                Sb = [P[0], P[1], P[5]]
                LA = 3
                pi = 0
                for qb in range(4):
                    a0 = [f512[0], f512[1]]
                    for c in range(2):
                        O0, O1, Db = P[2], P[3], P[4]
                        nk = 4 * qb + 4

                        def geom(kt):
                            q0 = max(kt * 128, qb * 512)
                            return q0, q0 - qb * 512

                        def S_emit(kt, base):
                            S = Sb[(base + kt) % 3]
                            q0, off = geom(kt)
                            mm(S.t[:, off:512], km[:, c, kt * 128:(kt + 1) * 128], qm[:, c, q0:(qb + 1) * 512], True, True, [big.k], [S.k])

                        def rest_emit(kt, base):
                            S = Sb[(base + kt) % 3]
                            pT = b512[(base + kt) % 4]
                            q0, off = geom(kt)
                            if off > 0:
                                fw.op("pool", lambda e: e.memset(pT.t[:, 0:off], 0.0), writes=[pT.k])
                            fw.op("act", lambda e: e.activation(out=pT.t[:, off:512], in_=S.t[:, off:512], func=AF.Exp, scale=SCALE), reads=[S.k], writes=[pT.k])
                            if kt * 128 >= qb * 512:
                                fw.op("dve", lambda e: e.tensor_tensor(out=pT.t[:, off:off + 128], in0=pT.t[:, off:off + 128], in1=tri1, op=ALU.mult),
                                      reads=[pT.k, tri.k], writes=[pT.k])
                            mm(O0.t[:], vh[:, kt, 0:128], pT.t[:], kt == 0, kt == nk - 1, [big.k, pT.k], [O0.k])
                            mm(O1.t[:], vh[:, kt, 128:256], pT.t[:], kt == 0, kt == nk - 1, [big.k, pT.k], [O1.k])
                            mm(Db.t[:], ones.t[:], pT.t[:], kt == 0, kt == nk - 1, [ones.k, pT.k], [Db.k])

                        s_em = 0
                        for kt in range(nk):
                            while s_em < min(nk, kt + LA):
                                S_emit(s_em, pi)
                                s_em += 1
                            rest_emit(kt, pi)
                        pi += nk
                        rden = f512[2]
                        osb = [f512[5], f512[6]]
                        fw.op("act", lambda e: e.activation(out=rden.t[:], in_=Db.t[:], func=AF.Ln), reads=[Db.k], writes=[rden.k])
                        fw.op("act", lambda e: e.activation(out=rden.t[:], in_=rden.t[:], func=AF.Exp, scale=-1.0), reads=[rden.k], writes=[rden.k])
                        fw.op("act", lambda e: e.activation(out=osb[0].t[:], in_=O0.t[:], func=AF.Copy), reads=[O0.k], writes=[osb[0].k])
                        fw.op("act", lambda e: e.activation(out=osb[1].t[:], in_=O1.t[:], func=AF.Copy), reads=[O1.k], writes=[osb[1].k])
                        if c == 0:
                            fw.op("dve", lambda e: e.tensor_tensor(out=a0[0].t[:], in0=osb[0].t[:], in1=rden.t[:], op=ALU.mult), reads=[osb[0].k, rden.k], writes=[a0[0].k])
                            fw.op("pool", lambda e: e.tensor_tensor(out=a0[1].t[:], in0=osb[1].t[:], in1=rden.t[:], op=ALU.mult), reads=[osb[1].k, rden.k], writes=[a0[1].k])
                        else:
                            fw.op("dve", lambda e: e.tensor_scalar(out=rden.t[:], in0=rden.t[:], scalar1=lamc.t[:, 4:5], scalar2=None, op0=ALU.mult),
                                  reads=[rden.k, lamc.k], writes=[rden.k])
                            for jj in range(2):
                                eng_ = "dve" if jj == 0 else "pool"
                                fw.op(eng_, lambda e: e.tensor_tensor(out=osb[jj].t[:], in0=osb[jj].t[:], in1=rden.t[:], op=ALU.mult), reads=[osb[jj].k, rden.k], writes=[osb[jj].k])
                                fw.op("pool", lambda e: e.tensor_tensor(out=a0[jj].t[:], in0=a0[jj].t[:], in1=osb[jj].t[:], op=ALU.add), reads=[a0[jj].k, osb[jj].k], writes=[a0[jj].k])
                    sqs = [b512[4], b512[5]]
                    for jj in range(2):
                        fw.op("act", lambda e: e.activation(out=sqs[jj].t[:], in_=a0[jj].t[:], func=AF.Square), reads=[a0[jj].k], writes=[sqs[jj].k])
                    Sq = P[5]
                    for jj in range(2):
                        mm(Sq.t[:], ones.t[:], sqs[jj].t[:], jj == 0, jj == 1, [ones.k, sqs[jj].k], [Sq.k])
                    rs = f512[4]
                    fw.op("act", lambda e: e.activation(out=rs.t[:], in_=Sq.t[:], func=AF.Ln, scale=1.0 / 256, bias=epsc.t[:, 0:1]), reads=[Sq.k], writes=[rs.k])
                    fw.op("act", lambda e: e.activation(out=rs.t[:], in_=rs.t[:], func=AF.Exp, scale=-0.5), reads=[rs.k], writes=[rs.k])
                    for jj in range(2):
                        ch = 2 * h + jj
                        fw.op("act", lambda e: e.activation(out=a0[jj].t[:], in_=a0[jj].t[:], func=AF.Identity, scale=gcol.t[:, 4 + jj:5 + jj]),
                              reads=[a0[jj].k, gcol.k], writes=[a0[jj].k])
                        fw.op("dve", lambda e: e.tensor_tensor(out=hT.t[:, ch, qb * 512:(qb + 1) * 512], in0=a0[jj].t[:], in1=rs.t[:], op=ALU.mult),
                              reads=[a0[jj].k, rs.k], writes=[hT_k[ch]])
            if stop == "diff_attn":
                return
            phase_outproj(l, diff_w_out[j])

        def dump_dbg():
            if not dbg:
                return
            fence()
            for tt in range(NT):
                b_ = xt[tt % 2]
                fw.dma("sp", b_.t[:], xres[tt * 128:(tt + 1) * 128, :], reads=xk[tt], writes=[b_.k])
                fw.dma("sp", dbg_out["dbg_x"][tt * 128:(tt + 1) * 128, :], b_.t[:], reads=[b_.k], is_output=True)
            for kc in range(KC):
                fw.dma("sp", dbg_out["dbg_h"][:, kc * T:(kc + 1) * T], hT.t[:, kc, :], reads=[hT_k[kc]], is_output=True)
            if stop not in ("ada", "norm0", "cols"):
                for i in range(16):
                    fw.dma("sp", dbg_out["dbg_q"][i], qT_s[i], reads=[qk[i]], is_output=True)
                    fw.dma("sp", dbg_out["dbg_k"][i], kT_s[i], reads=[kk[i]], is_output=True)
                for tt in range(NT):
                    _vc = 1024 if (stop or "").startswith("nsa") or stop in ("attn0", "mlp0") else 2048
                    fw.dma("sp", dbg_out["dbg_v"][tt * 128:(tt + 1) * 128, 0:_vc], v_s[tt * 128:(tt + 1) * 128, 0:_vc], reads=vk, is_output=True)

        for l in range(depth):
            if stop == "ada":
                break
            load_layer_cols(l)
            if stop == "cols":
                break
            phase_norm(l, 0)
            if stop == "norm%d" % l:
                break
            if l % 2 == 0:
                phase_nsa(l, l // 2)
            else:
                phase_diff(l, l // 2, 0.8 - 0.6 * math.exp(-0.3 * l))
            if stop in ("nsa_proj", "nsa_cmp", "nsa_attn", "diff_proj", "diff_attn") and ((l % 2 == 0) == stop.startswith("nsa")):
                break
            if stop == "attn%d" % l:
                break
            phase_norm(l, 1)
            phase_mlp(l, final=(l == depth - 1))
            if stop == "mlp%d" % l:
                break
        dump_dbg()
        fw.finish()
        print("instructions", fw.n_ins, "waits", fw.n_wait, "sems", fw.nsem, "sbuf_left", nc.sbuf_bytes_remaining)
    return nc


def kernel(**inputs):
    return run(inputs)


def run(inputs, depth=4, stop=None, dbg=False, ncores=None):
    consts = _consts()
    nc = build_program(depth=depth, stop=stop, dbg=dbg)
    B = inputs["x"].shape[0] if ncores is None else ncores
    shared = {k: np.ascontiguousarray(v) for k, v in inputs.items() if k not in ("x", "c", "positions")}
    in_maps = []
    for b in range(B):
        m = dict(shared)
        m["x"] = np.ascontiguousarray(inputs["x"][b])
        m["c"] = np.ascontiguousarray(inputs["c"][b])
        m["positions"] = np.ascontiguousarray(inputs["positions"][b]).astype(np.int32)
        m.update(consts)
        in_maps.append(m)
    res = run_bass_kernel_spmd(nc, in_maps, core_ids=list(range(B)))
    if dbg:
        return res.results
    return np.stack([np.asarray(r["out"], dtype=np.float32) for r in res.results], axis=0)
```
